# Optimizing a Trainium2 kernel written in Bass

```python
import jax
import jax.numpy as jnp
from jax import lax
import numpy as np

D_MODEL = 1024
BATCH = 8
SEQ = 2048
DEPTH = 4

CONV_K = 4
NORM_EPS = 1e-6
RG_WIDTH = 512
RG_BLOCKS = 8
RG_BLOCK = RG_WIDTH // RG_BLOCKS
RG_C = 8.0
ML_HEADS = 4
ML_DH = 128
ML_WIDTH = ML_HEADS * ML_DH
ML_CHUNK = 64
GD_HEADS = 4
GD_DK = 128
GD_DV = 128
GD_QK = GD_HEADS * GD_DK
GD_WIDTH = GD_HEADS * GD_DV
GD_CHUNK = 64
D_MIX = RG_WIDTH + ML_WIDTH + GD_WIDTH
IN_SIZES = (RG_WIDTH, RG_WIDTH,
            ML_WIDTH, ML_WIDTH, ML_WIDTH, ML_WIDTH, ML_WIDTH, ML_HEADS, ML_HEADS,
            GD_QK, GD_QK, GD_WIDTH, GD_WIDTH, GD_HEADS, GD_HEADS)
D_IN = sum(IN_SIZES)

kernel_name = "hymba_rglru_mlstm_gdn_trunk"


def rmsnorm(x, w):
    xf = x.astype(jnp.float32)
    r = lax.rsqrt(jnp.mean(xf * xf, axis=-1, keepdims=True) + NORM_EPS)
    return (xf * r).astype(x.dtype) * w


def _rms_f32(x):
    return x * lax.rsqrt(jnp.mean(x * x, axis=-1, keepdims=True) + NORM_EPS)


def _l2norm(x):
    return x * lax.rsqrt(jnp.sum(x * x, axis=-1, keepdims=True) + NORM_EPS)


def causal_dwconv(x, w):
    K = w.shape[0]
    S = x.shape[1]
    xp = jnp.pad(x, ((0, 0), (K - 1, 0), (0, 0)))
    y = xp[:, 0:S] * w[0]
    for k in range(1, K):
        y = y + xp[:, k:k + S] * w[k]
    return y


def _lin_combine(left, right):
    a_l, b_l = left
    a_r, b_r = right
    return a_l * a_r, a_r * b_l + b_r


def rglru_branch(xb, zb, conv_w, conv_b, gate_w, gate_b, lam):
    Bn, S, _ = xb.shape
    xc = causal_dwconv(xb, conv_w) + conv_b
    xblk = xc.reshape(Bn, S, RG_BLOCKS, RG_BLOCK)
    gates = jnp.einsum('bsnc,gncd->gbsnd', xblk, gate_w).reshape(2, Bn, S, RG_WIDTH)
    gates = gates + gate_b[:, None, None, :]
    r = jax.nn.sigmoid(gates[0])
    i = jax.nn.sigmoid(gates[1])
    log_a = -RG_C * r * jax.nn.softplus(-lam)
    a = jnp.exp(log_a)
    b = jnp.sqrt(-jnp.expm1(2.0 * log_a)) * (i * xc)
    _, h = lax.associative_scan(_lin_combine, (a, b), axis=1)
    return h * jax.nn.silu(zb)


def mlstm_branch(q, k, v, o_pre, z, i_pre, f_pre, gate_b, norm_w):
    Bn, S, _ = q.shape
    L = ML_CHUNK
    N = S // L

    def chunks(t):
        return t.reshape(Bn, N, L, ML_HEADS, ML_DH).transpose(1, 0, 3, 2, 4)

    def chunks_s(t):
        return t.reshape(Bn, N, L, ML_HEADS).transpose(1, 0, 3, 2)

    qc = chunks(q) * (ML_DH ** -0.5)
    kc = chunks(k)
    vc = chunks(v)
    li = chunks_s(i_pre + gate_b[0])
    lf = chunks_s(jax.nn.log_sigmoid(f_pre + gate_b[1]))
    bcum = jnp.cumsum(lf, axis=-1)
    causal = jnp.tril(jnp.ones((L, L), dtype=bool))
    dmat = jnp.where(causal, bcum[..., :, None] - bcum[..., None, :] + li[..., None, :], -jnp.inf)
    w_state = bcum[..., -1:] - bcum + li

    def step(carry, xs):
        C, n, m = carry
        dm, bq, ws, qi, ki, vi = xs
        m_inter = bq + m[..., None]
        m_t = jnp.maximum(m_inter, jnp.max(dm, axis=-1))
        p = jnp.exp(dm - m_t[..., None])
        s = jnp.einsum('bhtd,bhsd->bhts', qi, ki) * p
        sc = jnp.exp(m_inter - m_t)
        num = jnp.einsum('bhts,bhse->bhte', s, vi) + sc[..., None] * jnp.einsum('bhtd,bhde->bhte', qi, C)
        den = jnp.sum(s, axis=-1) + sc * jnp.einsum('bhtd,bhd->bht', qi, n)
        h = num / jnp.maximum(jnp.abs(den), jnp.exp(-m_t))[..., None]
        g = bq[..., -1]
        m_new = jnp.maximum(g + m, jnp.max(ws, axis=-1))
        dec = jnp.exp(g + m - m_new)
        wk = jnp.exp(ws - m_new[..., None])
        C = dec[..., None, None] * C + jnp.einsum('bhs,bhsd,bhse->bhde', wk, ki, vi)
        n = dec[..., None] * n + jnp.einsum('bhs,bhsd->bhd', wk, ki)
        return (C, n, m_new), h

    init = (jnp.zeros((Bn, ML_HEADS, ML_DH, ML_DH), jnp.float32),
            jnp.zeros((Bn, ML_HEADS, ML_DH), jnp.float32),
            jnp.zeros((Bn, ML_HEADS), jnp.float32))
    _, h = lax.scan(step, init, (dmat, bcum, w_state, qc, kc, vc))
    h = h.transpose(1, 0, 3, 2, 4).reshape(Bn, S, ML_HEADS, ML_DH)
    h = _rms_f32(h) * norm_w.reshape(ML_HEADS, ML_DH)
    h = h.reshape(Bn, S, ML_WIDTH)
    return h * jax.nn.sigmoid(o_pre) * jax.nn.silu(z)


def gdn_branch(q, k, v, z, a_pre, b_pre, conv_w, a_log, dt_bias, norm_w):
    Bn, S, _ = q.shape
    L = GD_CHUNK
    N = S // L
    qkv = jax.nn.silu(causal_dwconv(jnp.concatenate([q, k, v], axis=-1), conv_w))
    q, k, v = jnp.split(qkv, [GD_QK, 2 * GD_QK], axis=-1)

    def chunks(t, d):
        return t.reshape(Bn, N, L, GD_HEADS, d).transpose(0, 3, 1, 2, 4)

    def chunks_s(t):
        return t.reshape(Bn, N, L, GD_HEADS).transpose(0, 3, 1, 2)

    q = _l2norm(chunks(q, GD_DK)) * (GD_DK ** -0.5)
    k = _l2norm(chunks(k, GD_DK))
    v = chunks(v, GD_DV)
    beta = chunks_s(jax.nn.sigmoid(b_pre))
    g = chunks_s(-jnp.exp(a_log) * jax.nn.softplus(a_pre + dt_bias))
    gc = jnp.cumsum(g, axis=-1)
    incl = jnp.tril(jnp.ones((L, L), dtype=bool))
    strict = jnp.tril(jnp.ones((L, L), dtype=bool), k=-1)
    gam = jnp.exp(jnp.where(incl, gc[..., :, None] - gc[..., None, :], -jnp.inf))
    kb = k * beta[..., None]
    m_strict = jnp.where(strict, jnp.einsum('bhntd,bhnsd->bhnts', kb, k) * gam, 0.0)
    eye = jnp.eye(L, dtype=m_strict.dtype)
    t_inv = lax.linalg.triangular_solve(eye + m_strict, jnp.broadcast_to(eye, m_strict.shape),
                                        left_side=True, lower=True, unit_diagonal=True)
    u = t_inv @ (v * beta[..., None])
    w = t_inv @ (kb * jnp.exp(gc)[..., None])
    aqk = jnp.einsum('bhntd,bhnsd->bhnts', q, k) * gam
    q_dec = q * jnp.exp(gc)[..., None]
    g_last = gc[..., -1]
    k_dec = k * jnp.exp(g_last[..., None] - gc)[..., None]
    xs = (jnp.moveaxis(u, 2, 0), jnp.moveaxis(w, 2, 0), jnp.moveaxis(aqk, 2, 0),
          jnp.moveaxis(q_dec, 2, 0), jnp.moveaxis(k_dec, 2, 0), jnp.moveaxis(g_last, 2, 0))

    def step(state, xs_i):
        ui, wi, ai, qi, ki, gl = xs_i
        v_new = ui - wi @ state
        o = qi @ state + ai @ v_new
        state = state * jnp.exp(gl)[..., None, None] + jnp.swapaxes(ki, -1, -2) @ v_new
        return state, o

    _, o = lax.scan(step, jnp.zeros((Bn, GD_HEADS, GD_DK, GD_DV), jnp.float32), xs)
    o = o.transpose(1, 0, 3, 2, 4).reshape(Bn, S, GD_HEADS, GD_DV)
    o = _rms_f32(o) * norm_w * jax.nn.silu(z.reshape(Bn, S, GD_HEADS, GD_DV))
    return o.reshape(Bn, S, GD_WIDTH)


def setup_inputs(seed: int = 0) -> dict:
    key = jax.random.key(seed)
    ks = jax.random.split(key, 17)
    f32 = jnp.float32
    nrm = jax.random.normal
    x = nrm(ks[0], (BATCH, SEQ, D_MODEL), f32)
    norm_w = 1.0 + 0.02 * nrm(ks[1], (DEPTH, D_MODEL), f32)
    w_in = nrm(ks[2], (DEPTH, D_MODEL, D_IN), f32) * (D_MODEL ** -0.5)
    rg_conv_w = nrm(ks[3], (DEPTH, CONV_K, RG_WIDTH), f32) * (CONV_K ** -0.5)
    rg_conv_b = 0.01 * nrm(ks[4], (DEPTH, RG_WIDTH), f32)
    rg_gate_w = nrm(ks[5], (DEPTH, 2, RG_BLOCKS, RG_BLOCK, RG_BLOCK), f32) * (RG_BLOCK ** -0.5)
    rg_gate_b = 0.01 * nrm(ks[6], (DEPTH, 2, RG_WIDTH), f32)
    a_c = jax.random.uniform(ks[7], (DEPTH, RG_WIDTH), f32, minval=0.9, maxval=0.999)
    a0 = a_c ** (1.0 / RG_C)
    rg_lambda = jnp.log(a0) - jnp.log1p(-a0)
    ml_i_b = 0.1 * nrm(ks[8], (DEPTH, ML_HEADS), f32)
    ml_f_b = jnp.linspace(3.0, 6.0, ML_HEADS, dtype=f32)[None, :] + 0.1 * nrm(ks[9], (DEPTH, ML_HEADS), f32)
    ml_gate_b = jnp.stack([ml_i_b, ml_f_b], axis=1)
    ml_norm_w = 1.0 + 0.02 * nrm(ks[10], (DEPTH, ML_WIDTH), f32)
    gd_conv_w = nrm(ks[11], (DEPTH, CONV_K, 2 * GD_QK + GD_WIDTH), f32) * (CONV_K ** -0.5)
    gd_a_log = jnp.log(jax.random.uniform(ks[12], (DEPTH, GD_HEADS), f32, minval=1.0, maxval=16.0))
    dt = jnp.exp(jax.random.uniform(ks[13], (DEPTH, GD_HEADS), f32,
                                    minval=float(np.log(1e-3)), maxval=float(np.log(1e-1))))
    gd_dt_bias = dt + jnp.log(-jnp.expm1(-dt))
    gd_norm_w = 1.0 + 0.02 * nrm(ks[14], (DEPTH, GD_DV), f32)
    w_out = nrm(ks[15], (DEPTH, D_MIX, D_MODEL), f32) * (D_MIX ** -0.5)
    final_norm_w = 1.0 + 0.02 * nrm(ks[16], (D_MODEL,), f32)
    return {"x": x, "norm_w": norm_w, "w_in": w_in, "rg_conv_w": rg_conv_w, "rg_conv_b": rg_conv_b,
            "rg_gate_w": rg_gate_w, "rg_gate_b": rg_gate_b, "rg_lambda": rg_lambda,
            "ml_gate_b": ml_gate_b, "ml_norm_w": ml_norm_w, "gd_conv_w": gd_conv_w,
            "gd_a_log": gd_a_log, "gd_dt_bias": gd_dt_bias, "gd_norm_w": gd_norm_w,
            "w_out": w_out, "final_norm_w": final_norm_w}


def reference(x, norm_w, w_in, rg_conv_w, rg_conv_b, rg_gate_w, rg_gate_b, rg_lambda,
              ml_gate_b, ml_norm_w, gd_conv_w, gd_a_log, gd_dt_bias, gd_norm_w,
              w_out, final_norm_w):
    split_idx = [int(c) for c in np.cumsum(IN_SIZES)[:-1]]
    for l in range(DEPTH):
        hn = rmsnorm(x, norm_w[l])
        proj = (hn @ w_in[l]).astype(jnp.float32)
        (rg_x, rg_z, ml_q, ml_k, ml_v, ml_o, ml_z, ml_i, ml_f,
         gd_q, gd_k, gd_v, gd_z, gd_a, gd_b) = jnp.split(proj, split_idx, axis=-1)
        y_rg = rglru_branch(rg_x, rg_z, rg_conv_w[l], rg_conv_b[l], rg_gate_w[l], rg_gate_b[l], rg_lambda[l])
        y_ml = mlstm_branch(ml_q, ml_k, ml_v, ml_o, ml_z, ml_i, ml_f, ml_gate_b[l], ml_norm_w[l])
        y_gd = gdn_branch(gd_q, gd_k, gd_v, gd_z, gd_a, gd_b, gd_conv_w[l], gd_a_log[l],
                          gd_dt_bias[l], gd_norm_w[l])
        y = jnp.concatenate([y_rg, y_ml, y_gd], axis=-1)
        x = x + y.astype(x.dtype) @ w_out[l]
    return rmsnorm(x, final_norm_w)
```

```python
import math
from contextlib import ExitStack
import numpy as np
import concourse.bass as bass
import concourse.mybir as mybir
from concourse.bass_utils import run_bass_kernel_spmd

F32 = mybir.dt.float32
BF16 = mybir.dt.bfloat16
AF = mybir.ActivationFunctionType
ALU = mybir.AluOpType
AX = mybir.AxisListType

S = 2048
D = 1024
DEPTH = 4
NT = 4
EPS = 1e-6
NSLOT = 6
ENGS = ("pe", "act", "dve", "pool", "sp")


class V:
    def __init__(self, ap, key):
        self.ap = ap
        self.key = key

    def __getitem__(self, idx):
        return V(self.ap[idx], self.key)


def U(x):
    return x.ap if isinstance(x, V) else x


def KN(x):
    return x.key if isinstance(x, V) else x.name


class Prog:
    def __init__(self, nc):
        self.nc = nc
        self.stack = ExitStack()
        self.streams = {e: [] for e in ENGS}
        self.epoch = 0
        self.cnt = {}
        self.know = {e: {} for e in ENGS}
        self.last_w = {}
        self.readers = {}
        self.kn_at = {}
        self.dma_tot = {}
        self.ps_banks = []
        self.ps_i = 0
        self.tmps = {}
        self.nops = 0
        self.ps_reader = {}
        self.tags = {}
        self.tag = ""

    def sb(self, name, shape, dtype, stack=None):
        st = stack or self.stack
        self.uid = getattr(self, "uid", 0) + 1
        return st.enter_context(self.nc.sbuf_tensor(f"{name}_u{self.uid}", list(shape), dtype))

    def sb_once(self, name, shape, dtype):
        if not hasattr(self, "_once"):
            self._once = {}
        if name not in self._once:
            self._once[name] = self.sb(name, shape, dtype)
        return self._once[name]

    def init_psum(self):
        for i in range(8):
            self.ps_banks.append(self.stack.enter_context(self.nc.psum_tensor(f"psb{i}", [128, 512], F32)))

    def ps(self):
        t = self.ps_banks[self.ps_i % 8]
        self.ps_i += 1
        return t

    def tmp(self, stack, name, shape, dtype, bufs=2):
        key = name
        if key not in self.tmps or self.tmps[key][0] is not stack:
            self.tmps[key] = [stack, [self.sb(f"{name}_{i}", shape, dtype, stack) for i in range(bufs)], 0]
        ent = self.tmps[key]
        t = ent[1][ent[2] % len(ent[1])]
        ent[2] += 1
        return t

    def _deps(self, eng, reads, writes):
        deps = []
        for r in reads:
            if r.startswith("psb"):
                t = self.ps_reader.get(r)
                if t is not None and t[1] != eng:
                    deps.append(t)
        for r in reads:
            t = self.last_w.get(r)
            if t is not None:
                deps.append(t)
        for w in writes:
            t = self.last_w.get(w)
            if t is not None and not (t[0] == "eng" and t[1] == eng):
                deps.append(t)
            for t in self.readers.get(w, ()):
                if not (t[0] == "eng" and t[1] == eng):
                    deps.append(t)
        waits = []
        kn = self.know[eng]
        for t in deps:
            key, val = t[:3], t[3]
            if key[0] == "eng" and key[1] == eng and eng == "pe":
                continue
            if kn.get(key, 0) >= val:
                continue
            waits.append((key, val))
            for k2, v2 in self.kn_at[t].items():
                if kn.get(k2, 0) < v2:
                    kn[k2] = v2
            kn[key] = max(kn.get(key, 0), val)
        best = {}
        for key, val in waits:
            best[key] = max(best.get(key, 0), val)
        return list(best.items())

    def _commit(self, ticket, eng, reads, writes):
        snap = dict(self.know[eng])
        self.kn_at[ticket] = snap
        for r in reads:
            if r in writes:
                continue
            self.readers.setdefault(r, []).append(ticket)
        for w in writes:
            self.last_w[w] = ticket
            self.readers[w] = []

    def op(self, eng, fn, reads, writes):
        reads = [KN(a) for a in reads]
        writes = [KN(a) for a in writes]
        waits = self._deps(eng, reads, writes)
        k = (eng, self.epoch)
        self.cnt[k] = self.cnt.get(k, 0) + 1
        ticket = ("eng", eng, self.epoch, self.cnt[k])
        self.streams[eng].append((waits, fn, ("eng", eng, self.epoch), 1))
        self.tags.setdefault(eng, []).append(self.tag)
        self._commit(ticket, eng, reads, writes)
        for r in reads:
            if r.startswith("psb"):
                self.ps_reader[r] = ticket
        for w in writes:
            if w.startswith("psb"):
                self.ps_reader.pop(w, None)
        self.nops += 1
        return ticket

    def dma(self, eng, semkey, out, in_, extra_reads=()):
        reads = [in_.name] + [a.name for a in extra_reads]
        writes = [out.name]
        waits = self._deps(eng, reads, writes)
        prev = self.dma_tot.get(semkey, 0)
        key = ("dma", semkey, 0)
        if prev and self.know[eng].get(key, 0) < prev:
            waits.append((key, prev))
            self.know[eng][key] = prev
        tot = prev + 16
        self.dma_tot[semkey] = tot
        ticket = ("dma", semkey, 0, tot)

        def fn(e, out=out, in_=in_):
            return e.dma_start(out=out, in_=in_)
        self.streams[eng].append((waits, fn, key, 16))
        self._commit(ticket, eng, reads, writes)
        return ticket

    def barrier(self):
        latest = []
        for (e, ep), c in self.cnt.items():
            latest.append((("eng", e, ep), c))
        for e in ENGS:
            waits = []
            for key, val in latest:
                if key[1] == e and e == "pe":
                    continue
                if self.know[e].get(key, 0) < val:
                    waits.append((key, val))
                    self.know[e][key] = val
            if waits:
                self.streams[e].append((waits, None, None, 0))

    def finalize(self, final_waits):
        nc = self.nc
        sems = {}
        keys = set()
        for e in ENGS:
            for waits, fn, inc, n in self.streams[e]:
                for k, v in waits:
                    keys.add(k)
                if inc is not None:
                    keys.add(inc)
        for k in sorted(keys, key=str):
            sems[k] = self.stack.enter_context(nc.semaphore("s_" + "_".join(str(x) for x in k)))
        streams = self.streams

        def replay(e, eng):
            for waits, fn, inc, n in streams[e]:
                for k, v in waits:
                    eng.wait_ge(sems[k], v)
                if fn is not None:
                    ins = fn(eng)
                    ins.then_inc(sems[inc], n)
            if e == "sp":
                for k, v in final_waits:
                    eng.wait_ge(sems[k], v)

        with nc.Block() as block:
            @block.tensor
            def _(eng):
                replay("pe", eng)

            @block.scalar
            def _(eng):
                replay("act", eng)

            @block.vector
            def _(eng):
                replay("dve", eng)

            @block.gpsimd
            def _(eng):
                replay("pool", eng)

            @block.sync
            def _(eng):
                replay("sp", eng)

    def mm(self, out, lhsT, rhs, start=True, stop=True):
        o_, l_, r_ = U(out), U(lhsT), U(rhs)
        return self.op("pe", lambda e: e.matmul(o_, l_, r_, start=start, stop=stop), [lhsT, rhs], [out])

    def tr(self, out, in_, ident):
        o_, i_ = U(out), U(in_)
        return self.op("pe", lambda e: e.transpose(o_, i_, ident), [in_, ident], [out])

    def act(self, out, in_, func, bias=None, scale=None):
        kw = {}
        rd = [in_]
        if bias is not None:
            kw["bias"] = bias
            if not isinstance(bias, (int, float)):
                rd.append(bias)
        if scale is not None:
            kw["scale"] = scale
            if not isinstance(scale, (int, float)):
                rd.append(scale)
        o_, i_ = U(out), U(in_)
        return self.op("act", lambda e: e.activation(o_, i_, func, **kw), rd, [out])

    def tt(self, eng, out, in0, in1, op):
        o_, a_, b_ = U(out), U(in0), U(in1)
        return self.op(eng, lambda e: e.tensor_tensor(o_, a_, b_, op), [in0, in1], [out])

    def ts(self, eng, out, in0, s1, op0, s2=None, op1=None):
        rd = [in0] + [s for s in (s1, s2) if s is not None and not isinstance(s, (int, float))]
        if op1 is None:
            return self.op(eng, lambda e: e.tensor_scalar(out, in0, s1, None, op0), rd, [out])
        return self.op(eng, lambda e: e.tensor_scalar(out, in0, s1, s2, op0, op1), rd, [out])

    def stt(self, out, in0, scalar, in1, op0, op1):
        rd = [in0, in1] + ([] if isinstance(scalar, (int, float)) else [scalar])
        o_, a_, b_ = U(out), U(in0), U(in1)
        return self.op("dve", lambda e: e.scalar_tensor_tensor(o_, a_, scalar, b_, op0, op1), rd, [out])

    def copy(self, eng, out, in_):
        if eng == "act":
            return self.act(out, in_, AF.Copy)
        o_, i_ = U(out), U(in_)
        return self.op(eng, lambda e: e.tensor_copy(o_, i_), [in_], [out])

    def memset(self, eng, ap, val):
        return self.op(eng, lambda e: e.memset(ap, val), [], [ap])


L = 128
C_ID, C_ONES, C_U, C_SU, C_NBD, C_CU0, C_CU1, C_CU2, C_CL0, C_CL1, C_POS = range(11)
NCONST = 11


def _blockmask(bs):
    m = np.zeros((L, L), np.float32)
    for i in range(L // bs):
        m[i * bs:(i + 1) * bs, i * bs:(i + 1) * bs] = 1
    return m


def make_consts():
    U = np.triu(np.ones((L, L), np.float32))
    SU = np.triu(np.ones((L, L), np.float32), 1)
    bd = [_blockmask(b) for b in (16, 32, 64, 128)]
    c = np.zeros((NCONST, L, L), np.float32)
    c[C_ID] = np.eye(L)
    c[C_ONES] = 1.0
    c[C_U] = U
    c[C_SU] = SU
    c[C_NBD] = -bd[0] * SU
    for lv in range(3):
        c[C_CU0 + lv] = (bd[lv + 1] - bd[lv]) * SU
    for lv in range(2):
        c[C_CL0 + lv] = (bd[lv + 1] - bd[lv]) * SU.T
    c[C_POS] = 30000.0 * (1.0 - U)
    return np.ascontiguousarray(c.transpose(1, 0, 2).reshape(L, NCONST * L))


PP_NW = 0
PP_RCW = 8
PP_RCB = 24
PP_RGB = 28
PP_LAM = 36
PP_GCW = 40
PP_FNW = 88
PP_MLNW = 96
PP_GDNW = 100
NPP = 104
PB_MLB = 0
PB_DTB = 128
PB_ALOG = 192
NPB = 256


def pack_params(inp):
    pp = np.zeros((DEPTH, 128, NPP), np.float32)
    pb = np.zeros((DEPTH, 128, NPB), np.float32)
    gw = np.zeros((DEPTH, 8, 128, 128), np.float32)
    wg = np.zeros((DEPTH, D, 16), np.float32)
    for l in range(DEPTH):
        pp[l, :, PP_NW:PP_NW + 8] = inp["norm_w"][l].reshape(8, 128).T
        pp[l, :, PP_RCW:PP_RCW + 16] = inp["rg_conv_w"][l].reshape(4, 4, 128).transpose(2, 1, 0).reshape(128, 16)
        pp[l, :, PP_RCB:PP_RCB + 4] = inp["rg_conv_b"][l].reshape(4, 128).T
        pp[l, :, PP_RGB:PP_RGB + 8] = inp["rg_gate_b"][l].reshape(2, 4, 128).transpose(2, 0, 1).reshape(128, 8)
        pp[l, :, PP_LAM:PP_LAM + 4] = inp["rg_lambda"][l].reshape(4, 128).T
        pp[l, :, PP_GCW:PP_GCW + 48] = inp["gd_conv_w"][l].reshape(4, 12, 128).transpose(2, 1, 0).reshape(128, 48)
        pp[l, :, PP_FNW:PP_FNW + 8] = inp["final_norm_w"].reshape(8, 128).T
        pp[l, :, PP_MLNW:PP_MLNW + 4] = inp["ml_norm_w"][l].reshape(4, 128).T
        pp[l, :, PP_GDNW] = inp["gd_norm_w"][l]
        pb[l, :, PB_MLB:PB_MLB + 128] = np.tile(inp["ml_gate_b"][l].reshape(8), 16)[None]
        pb[l, :, PB_DTB:PB_DTB + 64] = np.tile(inp["gd_dt_bias"][l], 16)[None]
        pb[l, :, PB_ALOG:PB_ALOG + 64] = np.tile(inp["gd_a_log"][l], 16)[None]
        for g in range(2):
            for c in range(4):
                for h2 in range(2):
                    n = c * 2 + h2
                    gw[l, g * 4 + c, h2 * 64:(h2 + 1) * 64, h2 * 64:(h2 + 1) * 64] = inp["rg_gate_w"][l, g, n]
        wg[l, :, 0:8] = inp["w_in"][l][:, 3584:3592]
        wg[l, :, 8:16] = inp["w_in"][l][:, 5640:5648]
    return pp, pb, gw, wg


COLS = dict(rg_x=0, rg_z=512, ml_q=1024, ml_k=1536, ml_v=2048, ml_o=2560, ml_z=3072,
            gd_q=3592, gd_k=4104, gd_v=4616, gd_z=5128)


def build(n_layers, final_norm, phases=("rg", "ml", "gd")):
    nc = bass.Bass("TRN2", target_bir_lowering=False)
    x_d = nc.dram_tensor("x", [8, 128, S], F32, kind="ExternalInput").ap()
    win_d = nc.dram_tensor("w_in", [n_layers, D, 5648], F32, kind="ExternalInput").ap()
    wout_d = nc.dram_tensor("w_out", [n_layers, 1536, D], F32, kind="ExternalInput").ap()
    pp_d = nc.dram_tensor("pp", [n_layers, 128, NPP], F32, kind="ExternalInput").ap()
    pb_d = nc.dram_tensor("pb", [n_layers, 128, NPB], F32, kind="ExternalInput").ap()
    gw_d = nc.dram_tensor("gw", [n_layers, 8, 128, 128], F32, kind="ExternalInput").ap()
    wg_d = nc.dram_tensor("wg", [n_layers, D, 16], F32, kind="ExternalInput").ap()
    cst_d = nc.dram_tensor("cst", [128, NCONST * 128], F32, kind="ExternalInput").ap()
    y_d = nc.dram_tensor("y", [8, 128, S], F32, kind="ExternalOutput").ap()

    p = Prog(nc)
    with p.stack:
        p.init_psum()
        xT = [[p.sb(f"xT{c}_{t}", [128, 512], F32) for t in range(NT)] for c in range(8)]
        hn = [[p.sb(f"hn{c}_{t}", [128, 512], BF16) for t in range(NT)] for c in range(8)]
        ring = [p.sb(f"ring{i}", [128, 4096], BF16) for i in range(NSLOT)]
        FSET = [C_ID, C_ONES, C_U, C_SU, C_POS]
        BSET = [C_ID, C_ONES, C_NBD, C_CU0, C_CU1, C_CU2, C_CL0, C_CL1]
        cstf = p.sb("cstf", [128, len(FSET) * 128], F32)
        cstb = p.sb("cstb", [128, len(BSET) * 128], BF16)
        ii_b = p.sb("ii_b", [128, 256], BF16)
        ppt = [p.sb(f"pp{i}", [128, NPP], F32) for i in range(2)]
        pbt = [p.sb("pb0", [128, NPB], F32)] * 2
        gwt = [p.sb("gw0", [128, 8, 128], BF16)] * 2
        wgt = [p.sb("wg0", [128, 8, 16], BF16)] * 2

        def cf(i):
            j = FSET.index(i)
            return cstf[:, j * 128:(j + 1) * 128]

        def cb(i):
            j = BSET.index(i)
            return cstb[:, j * 128:(j + 1) * 128]

        with ExitStack() as st0:
            cst = p.sb("cst_stage", [128, NCONST * 128], F32, st0)
            p.dma("sp", "cst", cst[:], cst_d)
            for j, i in enumerate(FSET):
                p.copy("dve", cstf[:, j * 128:(j + 1) * 128], cst[:, i * 128:(i + 1) * 128])
            for j, i in enumerate(BSET):
                p.copy("dve", cstb[:, j * 128:(j + 1) * 128], cst[:, i * 128:(i + 1) * 128])
            p.copy("pool", ii_b[:, 0:128], cst[:, C_ID * 128:(C_ID + 1) * 128])
            p.copy("pool", ii_b[:, 128:256], cst[:, C_ID * 128:(C_ID + 1) * 128])
            p.barrier()
        for t in range(NT):
            for c in range(8):
                p.dma("sp", f"x{(c * NT + t) % 4}", xT[c][t][:], x_d[c, :, t * 512:(t + 1) * 512])

        ring_pos = [0]

        def load_group(l, kind, idx):
            slot = ring[ring_pos[0] % NSLOT]
            sk = f"ring{ring_pos[0] % NSLOT}"
            ring_pos[0] += 1
            if kind == "in":
                v = slot[:].rearrange("p (k c) -> p k c", k=8)
                src = win_d[l, :, idx:idx + 512].rearrange("(k p) c -> p k c", p=128)
                for half in range(2):
                    p.dma("pool", sk + f"_{half}", v[:, half * 4:(half + 1) * 4, :], src[:, half * 4:(half + 1) * 4, :])
                return v
            else:
                v = slot[:].rearrange("p (j m) -> p j m", j=4)
                src = wout_d[l, idx * 512:(idx + 1) * 512, :].rearrange("(j p) m -> p j m", p=128)
                for half in range(2):
                    p.dma("pool", sk + f"_{half}", v[:, half * 2:(half + 1) * 2, :], src[:, half * 2:(half + 1) * 2, :])
                return v

        def load_small(l):
            i = l % 2
            p.dma("sp", "pp", ppt[i][:], pp_d[l])

        def load_small2(l):
            p.dma("sp", "pb", pbt[0][:], pb_d[l])
            p.dma("pool", "gw", gwt[0][:], gw_d[l].rearrange("g p m -> p g m"))
            p.dma("pool", "wg", wgt[0][:], wg_d[l].rearrange("(k p) c -> p k c", p=128))

        def rms_to(dst, src_list_fn, pw_ap_fn, stack, fp32_out=False):
            for t in range(NT):
                ss = p.ps()
                for c in range(8):
                    sq = p.tmp(stack, "n_sq", [128, 512], BF16, 3)
                    p.act(sq[:], xT[c][t][:], AF.Square)
                    p.mm(ss[:], cb(C_ONES), sq[:], start=(c == 0), stop=(c == 7))
                lnv = p.tmp(stack, "n_ln", [128, 512], F32, 2)
                p.act(lnv[:], ss[:], AF.Ln, bias=EPS, scale=1.0 / D)
                rstd = p.tmp(stack, "n_rs", [128, 512], F32, 2)
                p.act(rstd[:], lnv[:], AF.Exp, scale=-0.5)
                for c in range(8):
                    p.stt(dst(c, t), xT[c][t][:], pw_ap_fn(c), rstd[:], ALU.mult, ALU.mult)

        def out_proj(Wo, yT, t, stk):
            for m in range(8):
                pso = p.ps()
                for j in range(4):
                    p.mm(pso[:], Wo[:, j, m * 128:(m + 1) * 128], yT[:, j, :], start=(j == 0), stop=(j == 3))
                tmpo = p.tmp(stk, "op_tmp", [128, 512], F32, 1)
                p.act(tmpo[:], pso[:], AF.Copy)
                p.tt("dve", xT[m][t][:], tmpo[:], xT[m][t][:], ALU.add)

        def out_proj_g(Wo, yT, t, tmpo):
            for m in range(8):
                pso = p.ps()
                for j in range(4):
                    p.mm(pso[:], Wo[:, j, m * 128:(m + 1) * 128], yT[:, j, :], start=(j == 0), stop=(j == 3))
                p.act(tmpo[:], pso[:], AF.Copy)
                p.tt("dve", xT[m][t][:], tmpo[:], xT[m][t][:], ALU.add)
                yield

        def chain_(*gs):
            for g in gs:
                yield from g

        def conv4(stack, psx, tail, wcol, bias_ap, name, bufs=1, xe_bufs=None):
            xe = p.tmp(stack, name + "_xe", [128, 515], F32, xe_bufs or bufs)
            p.copy("pool", xe[:, 0:3], tail[:])
            if bias_ap is not None:
                p.copy("dve", xe[:, 3:515], psx[:])
            else:
                p.act(xe[:, 3:515], psx[:], AF.Copy)
            p.copy("pool", tail[:], xe[:, 512:515])
            xc = p.tmp(stack, name + "_xc", [128, 512], F32, bufs)
            if bias_ap is not None:
                p.ts("dve", xc[:], xe[:, 3:515], wcol(3), ALU.mult, bias_ap, ALU.add)
            else:
                p.act(xc[:], psx[:], AF.Copy, scale=wcol(3))
            for k in range(3):
                p.stt(xc[:], xe[:, k:k + 512], wcol(k), xc[:], ALU.mult, ALU.add)
            conv4.xe = xe
            return xc

        def transpose_out(stack, y_tm, yT, tok, nw_cols):
            pst = p.ps()
            pv = pst[:].bitcast(BF16)
            for h in range(4):
                p.tr(pv[:, h * 128:(h + 1) * 128], y_tm[:, h * 128:(h + 1) * 128], cb(C_ID))
            if len(nw_cols) == 1:
                p.act(yT[:, :, tok], pv[:, 0:512].rearrange("p (h t) -> p h t", h=4), AF.Copy, scale=nw_cols[0])
            else:
                for h in range(4):
                    p.act(yT[:, h, tok], pv[:, h * 128:(h + 1) * 128], AF.Copy, scale=nw_cols[h])

        def h4(ap):
            return ap.rearrange("p (h d) -> p h d", h=4)

        def bce(ap4):
            return ap4.to_broadcast([128, 4, 128])

        def bch(ap128):
            return ap128.unsqueeze(1).to_broadcast([128, 4, 128])

        def run_(g):
            for _ in g:
                pass

        def weave_(g1, g2):
            a = b = True
            while a or b:
                if a:
                    try:
                        next(g1)
                    except StopIteration:
                        a = False
                if b:
                    try:
                        next(g2)
                    except StopIteration:
                        b = False

        def rg_phase(l, W):
            pp_ = ppt[l % 2]
            gw_ = gwt[l % 2]
            Wx, Wz, Wo = W
            with ExitStack() as st:
                tail = [p.sb(f"rg_tail{j}", [128, 3], F32, st) for j in range(4)]
                hprev = [p.sb(f"rg_hp{j}", [128, 1], F32, st) for j in range(4)]
                sm = p.sb("rg_sm", [128, 32], F32, st)
                for j in range(4):
                    p.memset("pool", tail[j][:], 0.0)
                    p.memset("pool", hprev[j][:], 0.0)
                lam = pp_[:, PP_LAM:PP_LAM + 4]
                yv, uv, lv, dv, rv, cfv, cf2v = (sm[:, 4 * i:4 * i + 4] for i in range(7))
                p.act(yv, lam, AF.Exp, scale=-1.0)
                p.ts("pool", uv, yv, 1.0, ALU.add)
                p.act(lv, uv, AF.Ln)
                p.ts("pool", dv, uv, -1.0, ALU.add, 1e-30, ALU.max)
                p.op("dve", lambda e: e.reciprocal(rv, dv), [dv], [rv])
                p.tt("pool", lv, lv, yv, ALU.mult)
                p.tt("pool", lv, lv, rv, ALU.mult)
                p.ts("pool", cfv, lv, -8.0, ALU.mult)
                p.ts("pool", cf2v, lv, -16.0, ALU.mult)
                yrgs = [p.sb(f"rg_y{i}", [128, 4, 512], BF16, st) for i in range(2)]
                rg_tmpo = p.sb("rg_tmpo", [128, 512], F32, st)

                def gen_a(t, jp, out):
                    us = []
                    pss = []
                    for j in (2 * jp, 2 * jp + 1):
                        js = slice(j * 128, (j + 1) * 128)
                        psx = p.ps()
                        for kc in range(8):
                            p.mm(psx[:], Wx[:, kc, js], hn[kc][t][:], start=(kc == 0), stop=(kc == 7))
                        pss.append(psx)
                    for j in (2 * jp, 2 * jp + 1):
                        js = slice(j * 128, (j + 1) * 128)
                        psz = p.ps()
                        for kc in range(8):
                            p.mm(psz[:], Wz[:, kc, js], hn[kc][t][:], start=(kc == 0), stop=(kc == 7))
                        pss.append(psz)
                    for i_, j in enumerate((2 * jp, 2 * jp + 1)):
                        xc = conv4(st, pss[i_], tail[j], lambda k, j=j: pp_[:, PP_RCW + j * 4 + k:PP_RCW + j * 4 + k + 1],
                                   pp_[:, PP_RCB + j:PP_RCB + j + 1], "rg", 4, 2)
                        us.append(dict(t=t, j=j, xc=xc))
                    for i_, u in enumerate(us):
                        u["sz"] = p.tmp(st, "rg_szb", [128, 512], BF16, 4)
                        p.act(u["sz"][:], pss[2 + i_][:], AF.Silu)
                    yield
                    for u in us:
                        j = u["j"]
                        xcb = p.tmp(st, "rg_xcb", [128, 512], BF16, 2)
                        p.copy("dve", xcb[:], u["xc"][:])
                        psr = p.ps()
                        p.mm(psr[:], gw_[:, j, :], xcb[:])
                        psi = p.ps()
                        p.mm(psi[:], gw_[:, 4 + j, :], xcb[:])
                        u["r"] = p.tmp(st, "rg_r", [128, 512], F32, 4)
                        p.act(u["r"][:], psr[:], AF.Sigmoid, bias=pp_[:, PP_RGB + j:PP_RGB + j + 1])
                        u["i"] = p.tmp(st, "rg_i", [128, 512], F32, 4)
                        p.act(u["i"][:], psi[:], AF.Sigmoid, bias=pp_[:, PP_RGB + 4 + j:PP_RGB + 4 + j + 1])
                        yield
                    out.extend(us)

                def gen_b(us):
                    for u in us:
                        j = u["j"]
                        u["a"] = p.tmp(st, "rg_a", [128, 512], F32, 2)
                        p.act(u["a"][:], u["r"][:], AF.Exp, scale=sm[:, 20 + j:21 + j])
                        u["a2"] = u["r"]
                        p.act(u["a2"][:], u["r"][:], AF.Exp, scale=sm[:, 24 + j:25 + j])
                        p.tt("dve", u["i"][:], u["i"][:], u["xc"][:], ALU.mult)
                    yield
                    for u in us:
                        p.act(u["a2"][:], u["a2"][:], AF.Sqrt, bias=1.0, scale=-1.0)
                        p.tt("pool", u["i"][:], u["i"][:], u["a2"][:], ALU.mult)
                    yield
                    for u in us:
                        j = u["j"]
                        hh = p.tmp(st, "rg_h", [128, 512], F32, 2)
                        u["h"] = hh
                        hp = hprev[j]
                        a = u["a"]
                        ig = u["i"]
                        p.op("dve", lambda e, hh=hh, a=a, ig=ig, hp=hp: e.tensor_tensor_scan(
                            hh[:], a[:], ig[:], hp[:], ALU.mult, ALU.add), [a, ig, hp], [hh])
                        p.copy("pool", hp[:], hh[:, 511:512])
                    yield
                    for u in us:
                        p.tt("dve", yrgs[u["t"] % 2][:, u["j"], :], u["h"][:], u["sz"][:], ALU.mult)
                    yield

                pairs = [(t, jp) for t in range(NT) for jp in range(2)]
                prev = None
                for (t, jp) in pairs:
                    p.tag = f"rg:{t}"
                    cur = []
                    if prev is None:
                        run_(gen_a(t, jp, cur))
                    elif prev[0]["j"] == 2:
                        weave_(gen_a(t, jp, cur),
                               chain_(gen_b(prev), out_proj_g(Wo, yrgs[prev[0]["t"] % 2], prev[0]["t"], rg_tmpo)))
                    else:
                        weave_(gen_a(t, jp, cur), gen_b(prev))
                    prev = cur
                run_(gen_b(prev))
                out_proj(Wo, yrgs[prev[0]["t"] % 2], prev[0]["t"], st)
                p.barrier()

        def ml_phase(l, W):
            ppm = ppt[l % 2]
            pb_ = pbt[l % 2]
            wg_ = wgt[l % 2]
            Wq, Wk, Wv, Wo_, Wz, Wout = W
            with ExitStack() as st:
                Cst = p.sb("ml_C", [128, 512], F32, st)
                Cb = p.sb("ml_Cb", [128, 512], BF16, st)
                nst = p.sb("ml_n", [128, 4], F32, st)
                nb = p.sb("ml_nb", [128, 4], BF16, st)
                p.memset("pool", Cst[:], 0.0)
                p.memset("pool", Cb[:], 0.0)
                p.memset("pool", nst[:], 0.0)
                p.memset("pool", nb[:], 0.0)
                v3 = lambda ap: ap.rearrange("p (c g) -> p c g", g=8)
                glA = p.sb("ml_glA", [128, 128], F32, st)
                enA = p.sb("ml_enA", [128, 16, 4], F32, st)
                lfnA = p.sb("ml_lfnA", [128, 16, 4], F32, st)
                ebgA = p.sb("ml_ebgA", [128, 128], F32, st)
                lwA = p.sb("ml_lwA", [128, 16, 4], F32, st)
                ewA = p.sb("ml_ewA", [128, 16, 4], F32, st)
                ewbA = p.sb("ml_ewbA", [128, 16, 4], BF16, st)
                psgA = p.ps()
                for c in range(16):
                    tok_ = slice((c % 4) * 128, (c % 4 + 1) * 128)
                    for kc in range(8):
                        p.mm(psgA[:, c * 8:(c + 1) * 8], hn[kc][c // 4][:, tok_], wg_[:, kc, 0:8],
                             start=(kc == 0), stop=(kc == 7))
                p.tt("dve", glA[:], psgA[:, 0:128], pb_[:, PB_MLB:PB_MLB + 128], ALU.add)
                p.act(enA[:], v3(glA[:])[:, :, 4:8], AF.Exp, scale=-1.0)
                p.act(lfnA[:], enA[:], AF.Ln, bias=1.0)
                pscA = p.ps()
                for c in range(16):
                    p.mm(pscA[:, c * 8:c * 8 + 4], cf(C_U), lfnA[:, c, :])
                    p.mm(pscA[:, c * 8 + 4:c * 8 + 8], cf(C_ONES), lfnA[:, c, :])
                p.act(ebgA[:], pscA[:, 0:128], AF.Exp, scale=-1.0)
                p.tt("dve", lwA[:], v3(pscA[:, 0:128])[:, :, 0:4], v3(glA[:])[:, :, 0:4], ALU.add)
                p.act(ewA[:], lwA[:], AF.Exp)
                p.copy("pool", ewbA[:], ewA[:])
                qTs = [[p.sb(f"ml_qT{h}_{i}", [128, 512], BF16, st) for h in range(4)] for i in range(2)]
                kTs = [[p.sb(f"ml_kT{h}_{i}", [128, 512], BF16, st) for h in range(4)] for i in range(2)]
                yT = p.sb("ml_yT", [128, 4, 512], BF16, st)

                def ml_feat(t):
                    p.tag = f"ml_feat:{t}"
                    for h in range(4):
                        hs = slice(h * 128, (h + 1) * 128)
                        for (Wm, dstl, sc) in ((Wq, qTs[t % 2], 128 ** -0.5), (Wk, kTs[t % 2], 1.0)):
                            psx = p.ps()
                            for kc in range(8):
                                p.mm(psx[:], Wm[:, kc, hs], hn[kc][t][:], start=(kc == 0), stop=(kc == 7))
                            p.act(dstl[h][:], psx[:], AF.Copy, scale=sc)
                            yield

                def ml_proj(t, cc, out):
                    tok = slice(cc * 128, (cc + 1) * 128)
                    p.tag = f"ml_proj:{t * 4 + cc}"
                    def tokproj(Wm, ncol=512, c0=0):
                        ps_ = p.ps()
                        for kc in range(8):
                            p.mm(ps_[:, 0:ncol], hn[kc][t][:, tok], Wm[:, kc, c0:c0 + ncol],
                                 start=(kc == 0), stop=(kc == 7))
                        return ps_
                    pstk = p.ps()
                    pvk = pstk[:].bitcast(BF16)
                    for h in range(4):
                        p.tr(pvk[:, h * 128:(h + 1) * 128], kTs[t % 2][h][:, tok], cb(C_ID))
                    ktm = p.tmp(st, "ml_ktm", [128, 512], BF16, 4)
                    p.act(ktm[:], pvk[:, 0:512], AF.Copy)
                    yield
                    pso = tokproj(Wo_)
                    so = p.tmp(st, "ml_so", [128, 512], BF16, 4)
                    p.act(so[:], pso[:], AF.Tanh, scale=0.5)
                    yield
                    psz = tokproj(Wz)
                    sz = p.tmp(st, "ml_sz", [128, 512], F32, 2)
                    p.act(sz[:], psz[:], AF.Silu)
                    p.stt(so[:], so[:], 1.0, sz[:], ALU.add, ALU.mult)
                    yield
                    c = t * 4 + cc
                    psv = tokproj(Wv)
                    vaug = p.tmp(st, "ml_vaug", [128, 512], BF16, 4)
                    p.tt("dve", h4(vaug[:]), h4(psv[:]), bce(ewA[:, c, :]), ALU.mult)
                    out.update(dict(t=t, cc=cc, c=c, tok=tok, ktm=ktm, so=so, vaug=vaug))
                    yield

                def ml_core(u):
                    t, cc, c, tok, ktm, so, vaug = (u[k] for k in ("t", "cc", "c", "tok", "ktm", "so", "vaug"))
                    qT = qTs[t % 2]
                    kT = kTs[t % 2]
                    sm = p.tmp(st, "ml_sm", [128, 64], F32, 2)
                    den, dd, rr = (sm[:, 4 * i:4 * i + 4] for i in (6, 7, 8))
                    eb = ebgA[:, c * 8:c * 8 + 4]
                    eg = ebgA[:, c * 8 + 4:c * 8 + 8]
                    p.tag = f"ml_core:{t * 4 + cc}"
                    psp = p.ps()
                    PT = p.tmp(st, "ml_PT", [128, 512], BF16, 1)
                    for h in range(4):
                        hs = slice(h * 128, (h + 1) * 128)
                        p.mm(psp[:, hs], kT[h][:, tok], qT[h][:, tok])
                    p.tt("dve", h4(PT[:]), h4(psp[:]), bch(cf(C_U)), ALU.mult)
                    yield
                    psn = p.ps()
                    psd = p.ps()
                    for h in range(4):
                        hs = slice(h * 128, (h + 1) * 128)
                        p.mm(psn[:, hs], PT[:, hs], vaug[:, hs], start=True, stop=False)
                        p.mm(psn[:, hs], qT[h][:, tok], Cb[:, hs], start=False, stop=True)
                    for h in range(4):
                        hs = slice(h * 128, (h + 1) * 128)
                        p.mm(psd[:, h:h + 1], PT[:, hs], ewbA[:, c, h:h + 1], start=True, stop=False)
                        p.mm(psd[:, h:h + 1], qT[h][:, tok], nb[:, h:h + 1], start=False, stop=True)
                    p.tt("dve", den, psd[:, 0:4], eb, ALU.mult)
                    p.ts("dve", dd, den, -1.0, ALU.mult)
                    p.tt("dve", dd, dd, den, ALU.max)
                    p.ts("dve", dd, dd, 1.0, ALU.max)
                    p.op("dve", lambda e, dd=dd: e.reciprocal(dd, dd), [dd], [dd])
                    p.tt("dve", rr, dd, eb, ALU.mult)
                    hh = p.tmp(st, "ml_hh", [128, 512], F32, 1)
                    p.tt("dve", h4(hh[:]), h4(psn[:]), bce(rr), ALU.mult)
                    yield
                    hsq = p.tmp(st, "ml_hsq", [128, 512], F32, 1)
                    p.act(hsq[:], hh[:], AF.Square)
                    ssq = sm[:, 36:40]
                    p.op("dve", lambda e, ssq=ssq, hsq=hsq: e.tensor_reduce(
                        ssq, hsq[:].rearrange("p (h d) -> p h d", h=4), AX.X, ALU.add), [hsq], [ssq])
                    lnv = sm[:, 40:44]
                    rstd = sm[:, 44:48]
                    p.act(lnv, ssq, AF.Ln, bias=EPS, scale=1.0 / 128)
                    p.act(rstd, lnv, AF.Exp, scale=-0.5, bias=math.log(0.5))
                    ytm = p.tmp(st, "ml_ytm", [128, 512], BF16, 1)
                    for h in range(4):
                        hs = slice(h * 128, (h + 1) * 128)
                        p.stt(ytm[:, hs], hh[:, hs], sm[:, 44 + h:45 + h], so[:, hs], ALU.mult, ALU.mult)
                    yield
                    transpose_out(st, ytm, yT, tok, [ppm[:, PP_MLNW + h:PP_MLNW + h + 1] for h in range(4)])
                    yield
                    p.tag = f"ml_state:{t * 4 + cc}"
                    psk2 = p.ps()
                    psn2 = p.ps()
                    for h in range(4):
                        hs = slice(h * 128, (h + 1) * 128)
                        p.mm(psk2[:, hs], ktm[:, hs], vaug[:, hs])
                    for h in range(4):
                        hs = slice(h * 128, (h + 1) * 128)
                        p.mm(psn2[:, h:h + 1], ktm[:, hs], ewbA[:, c, h:h + 1])
                    ctmp = p.tmp(st, "ml_ctmp", [128, 512], F32, 1)
                    p.tt("dve", ctmp[:], psk2[:], Cst[:], ALU.add)
                    p.tt("dve", h4(Cst[:]), h4(ctmp[:]), bce(eg), ALU.mult)
                    p.act(Cb[:], Cst[:], AF.Copy)
                    ntmp = sm[:, 48:52]
                    p.tt("dve", ntmp, psn2[:, 0:4], nst[:], ALU.add)
                    p.tt("dve", nst[:], ntmp, eg, ALU.mult)
                    p.copy("pool", nb[:], nst[:])
                    yield

                def chain(*gs):
                    for g in gs:
                        yield from g

                def cores(us, t, last):
                    for u in us:
                        yield from ml_core(u)
                    if last:
                        out_proj(Wout, yT, t, st)
                        yield

                halves = [(t, hf) for t in range(NT) for hf in range(2)]
                prev = None
                for (t, hf) in halves:
                    cur = [dict(), dict()]
                    gs = []
                    if hf == 0:
                        gs.append(ml_feat(t))
                    gs += [ml_proj(t, 2 * hf, cur[0]), ml_proj(t, 2 * hf + 1, cur[1])]
                    if prev is None:
                        run_(chain(*gs))
                    else:
                        weave_(chain(*gs), cores(prev[0], prev[1], prev[2]))
                    prev = (cur, t, hf == 1)
                run_(cores(prev[0], prev[1], prev[2]))
                p.barrier()

        def gd_inverse(st, Nb, TpT):
            H = range(len(Nb))
            A = [p.tmp(st, f"gi_A{h}", [128, 256], BF16, 1) for h in H]
            NTt = [p.tmp(st, f"gi_NT{h}", [128, 128], BF16, 1) for h in H]
            DD = [p.tmp(st, f"gi_DD{h}", [128, 256], BF16, 1) for h in H]
            for h in H:
                p.tt("pool", A[h][:, 0:128], Nb[h][:], cb(C_NBD), ALU.mult)
            pst = [p.ps() for h in H]
            for h in H:
                pv = pst[h][:].bitcast(BF16)
                p.tr(pv[:, 0:128], A[h][:, 0:128], cb(C_ID))
                p.tr(pv[:, 128:256], Nb[h][:], cb(C_ID))
            for h in H:
                pv = pst[h][:].bitcast(BF16)
                p.act(A[h][:, 128:256], pv[:, 0:128], AF.Copy)
                p.copy("dve", NTt[h][:], pv[:, 128:256])
                p.tt("pool", DD[h][:], A[h][:], ii_b[:], ALU.add)
            yield
            def sq_mm(Pc):
                psq = [p.ps() for h in H]
                for h in H:
                    p.mm(psq[h][:, 0:128], Pc[h][:, 128:256], Pc[h][:, 0:128])
                    p.mm(psq[h][:, 128:256], Pc[h][:, 0:128], Pc[h][:, 128:256])
                return psq

            def sq_evac(k, psq):
                Pn = [p.tmp(st, f"gi_P{k % 2}_{h}", [128, 256], BF16, 1) for h in H]
                IP = [p.tmp(st, f"gi_IP{h}", [128, 256], BF16, 1) for h in H]
                for h in H:
                    if k < 3:
                        p.act(Pn[h][:], psq[h][:, 0:256], AF.Copy)
                    p.tt("dve", IP[h][:], psq[h][:, 0:256], ii_b[:], ALU.add)
                return Pn, IP
            psq = sq_mm(A)
            Pn, IP = sq_evac(1, psq)
            yield
            for k in range(1, 4):
                psd = [p.ps() for h in H]
                for h in H:
                    p.mm(psd[h][:, 0:128], IP[h][:, 128:256], DD[h][:, 0:128])
                    p.mm(psd[h][:, 128:256], IP[h][:, 0:128], DD[h][:, 128:256])
                if k < 3:
                    psq = sq_mm(Pn)
                for h in H:
                    if h % 2 == 0:
                        p.act(DD[h][:], psd[h][:, 0:256], AF.Copy)
                    else:
                        p.copy("dve", DD[h][:], psd[h][:, 0:256])
                if k < 3:
                    Pn, IP = sq_evac(k + 1, psq)
                yield
            for lv in range(3):
                Cm = [p.tmp(st, f"gi_IP{h}", [128, 256], BF16, 1) for h in H]
                for h in H:
                    p.tt("pool", Cm[h][:, 0:128], Nb[h][:], cb(C_CU0 + lv), ALU.mult)
                    if lv < 2:
                        p.tt("pool", Cm[h][:, 128:256], NTt[h][:], cb(C_CL0 + lv), ALU.mult)
                psw = [p.ps() for h in H]
                for h in H:
                    p.mm(psw[h][:, 0:128], Cm[h][:, 0:128], DD[h][:, 128:256])
                    if lv < 2:
                        p.mm(psw[h][:, 128:256], Cm[h][:, 128:256], DD[h][:, 0:128])
                WW = [p.tmp(st, f"gi_P0_{h}", [128, 256], BF16, 1) for h in H]
                n = 256 if lv < 2 else 128
                for h in H:
                    p.act(WW[h][:, 0:n], psw[h][:, 0:n], AF.Copy)
                yield
                psz = [p.ps() for h in H]
                for h in H:
                    p.mm(psz[h][:, 0:128], WW[h][:, 0:128], DD[h][:, 0:128])
                    if lv < 2:
                        p.mm(psz[h][:, 128:256], WW[h][:, 128:256], DD[h][:, 128:256])
                for h in H:
                    if lv < 2:
                        p.tt("dve", DD[h][:], DD[h][:], psz[h][:, 0:256], ALU.subtract)
                    else:
                        p.tt("dve", TpT[h], DD[h][:, 0:128], psz[h][:, 0:128], ALU.subtract)
                yield

        def gd_phase(l, W, scratch):
            pp_ = ppt[l % 2]
            pb_ = pbt[l % 2]
            wg_ = wgt[l % 2]
            Wq, Wk, Wv, Wz, Wout = W
            with ExitStack() as st:
                Sst = p.sb("gd_S", [128, 512], F32, st)
                Sb = p.sb("gd_Sb", [128, 512], BF16, st)
                p.memset("pool", Sst[:], 0.0)
                p.memset("pool", Sb[:], 0.0)
                tails = [p.sb(f"gd_tail{i}", [128, 3], F32, st) for i in range(12)]
                for i in range(12):
                    p.memset("pool", tails[i][:], 0.0)
                v3 = lambda ap: ap.rearrange("p (c g) -> p c g", g=8)
                w3 = lambda ap: ap.rearrange("p (c g) -> p c g", g=4)
                expA = p.sb("gd_expA", [128, 16, 4], F32, st)
                p.act(expA[:], w3(pb_[:, PB_ALOG:PB_ALOG + 64]), AF.Exp)
                betaA = p.sb("gd_betaA", [128, 16, 4], F32, st)
                xxA = p.sb("gd_xxA", [128, 16, 4], F32, st)
                axA = p.sb("gd_axA", [128, 16, 4], F32, st)
                enA = p.sb("gd_enA", [128, 16, 4], F32, st)
                gnegA = p.sb("gd_gnegA", [128, 16, 4], F32, st)
                csA = p.sb("gd_csA", [128, 128], F32, st)
                egA = p.sb("gd_egA", [128, 128], F32, st)
                ekdA = p.sb("gd_ekdA", [128, 16, 4], F32, st)
                negegcA = p.sb("gd_negegcA", [128, 16, 4], F32, st)
                psgA = p.ps()
                for c in range(16):
                    tok_ = slice((c % 4) * 128, (c % 4 + 1) * 128)
                    for kc in range(8):
                        p.mm(psgA[:, c * 8:(c + 1) * 8], hn[kc][c // 4][:, tok_], wg_[:, kc, 8:16],
                             start=(kc == 0), stop=(kc == 7))
                p.act(betaA[:], v3(psgA[:, 0:128])[:, :, 4:8], AF.Sigmoid)
                p.tt("dve", xxA[:], v3(psgA[:, 0:128])[:, :, 0:4], w3(pb_[:, PB_DTB:PB_DTB + 64]), ALU.add)
                p.ts("dve", axA[:], xxA[:], -1.0, ALU.mult)
                p.tt("dve", axA[:], axA[:], xxA[:], ALU.max)
                p.act(enA[:], axA[:], AF.Exp, scale=-1.0)
                p.act(enA[:], enA[:], AF.Ln, bias=1.0)
                p.ts("dve", axA[:], xxA[:], 0.0, ALU.max)
                p.tt("dve", enA[:], enA[:], axA[:], ALU.add)
                p.tt("dve", gnegA[:], enA[:], expA[:], ALU.mult)
                pscA = p.ps()
                for c in range(16):
                    p.mm(pscA[:, c * 8:c * 8 + 4], cf(C_U), gnegA[:, c, :])
                    p.mm(pscA[:, c * 8 + 4:c * 8 + 8], cf(C_ONES), gnegA[:, c, :])
                p.act(csA[:], pscA[:, 0:128], AF.Copy)
                p.act(egA[:], pscA[:, 0:128], AF.Exp, scale=-1.0)
                p.tt("pool", ekdA[:], v3(csA[:])[:, :, 0:4], v3(csA[:])[:, :, 4:8], ALU.subtract)
                p.act(ekdA[:], ekdA[:], AF.Exp)
                p.ts("pool", negegcA[:], v3(egA[:])[:, :, 0:4], -1.0, ALU.mult)
                yT = p.sb("gd_yT", [128, 4, 512], BF16, st)
                gd_tmpo = p.sb("gd_tmpo", [128, 512], F32, st)
                qkT = [p.sb(f"gd_qkT{h}", [128, 2, 512], BF16, st) for h in range(4)]
                vT = [p.sb(f"gd_vT{h}", [128, 512], BF16, st) for h in range(4)]
                H4 = range(4)
                HS = [slice(h * 128, (h + 1) * 128) for h in H4]

                def cbufs(c):
                    b = c % 2
                    base = b * 2048
                    mk = lambda nm, o, n: V(scratch[:, base + o:base + o + n], f"gdscr_{nm}_{b}")
                    return dict(kdec=mk("kdec", 0, 512), vtm=mk("vtm", 512, 512), aqkT=mk("aqkT", 1024, 512),
                                TpT=[mk(f"T{h}", 1536 + h * 128, 128) for h in H4])

                def gd_zburst(sc_, t):
                    pzs = []
                    for cc in range(4):
                        tok = slice(cc * 128, (cc + 1) * 128)
                        psz = p.ps()
                        for kc in range(8):
                            p.mm(psz[:], hn[kc][t][:, tok], Wz[:, kc, :], start=(kc == 0), stop=(kc == 7))
                        pzs.append(psz)
                    szs = []
                    for cc in range(4):
                        sz = p.tmp(sc_, "gd_sz", [128, 512], BF16, 4)
                        p.act(sz[:], pzs[cc][:], AF.Silu)
                        szs.append(sz)
                    return szs

                def gd_part1(sc_, t, cc, out, sz):
                    tok = slice(cc * 128, (cc + 1) * 128)
                    c = t * 4 + cc
                    B = cbufs(c)
                    p.tag = f"gd_prep:{c}"
                    pst = p.ps()
                    pv = pst[:].bitcast(BF16)
                    for h in H4:
                        p.tr(pv[:, HS[h]], qkT[h][:, 0, tok], cb(C_ID))
                    kdec = B["kdec"]
                    p.tt("dve", V(h4(kdec.ap), kdec.key), h4(pv[:, 0:512]), bce(ekdA[:, c, :]), ALU.mult)
                    yield
                    pst2 = p.ps()
                    pv2 = pst2[:].bitcast(BF16)
                    for h in H4:
                        p.tr(pv2[:, HS[h]], vT[h][:, tok], cb(C_ID))
                    vtm = B["vtm"]
                    p.copy("dve", vtm, pv2[:, 0:512])
                    yield
                    p.tag = f"gd_gam:{c}"
                    GU = p.tmp(sc_, "gd_GU", [128, 512], F32, 1)
                    p.tt("dve", h4(GU[:]), bch(cf(C_U)), bce(gnegA[:, c, :]), ALU.mult)
                    psG = p.ps()
                    for h in H4:
                        p.mm(psG[:, HS[h]], cf(C_ID), cf(C_POS), start=True, stop=False)
                        p.mm(psG[:, HS[h]], cf(C_ONES), GU[:, HS[h]], start=False, stop=True)
                    gamT = p.tmp(sc_, "gd_gamT", [128, 512], F32, 1)
                    for h in H4:
                        p.act(gamT[:, HS[h]], psG[:, HS[h]], AF.Exp, scale=-1.0, bias=csA[:, c * 8 + h:c * 8 + h + 1])
                    yield
                    gSU = gamT
                    aqkT = B["aqkT"]
                    Nb = [p.tmp(sc_, f"gd_Nb{h}", [128, 128], BF16, 1) for h in H4]
                    for h in H4:
                        psK = p.ps()
                        p.mm(psK[:, 0:256].rearrange("p (a t) -> p a t", a=2), qkT[h][:, 0, tok], qkT[h][:, :, tok])
                        p.tt("dve", aqkT[:, HS[h]], psK[:, 128:256], gamT[:, HS[h]], ALU.mult)
                        p.stt(Nb[h][:], psK[:, 0:128], betaA[:, c, h:h + 1], gSU[:, HS[h]], ALU.mult, ALU.mult)
                    yield
                    p.tag = f"gd_inv:{c}"
                    yield from gd_inverse(sc_, Nb, B["TpT"])
                    out.update(B)
                    out.update(sz=sz, t=t, cc=cc, c=c, tok=tok)

                def gd_part2(sc_, u, yT):
                    t, cc, c, tok = u["t"], u["cc"], u["c"], u["tok"]
                    kdec, vtm, aqkT, TpT, sz = u["kdec"], u["vtm"], u["aqkT"], u["TpT"], u["sz"]
                    p.tag = f"gd_state:{c}"
                    psAk = p.ps()
                    psAq = p.ps()
                    for h in H4:
                        p.mm(psAk[:, HS[h]], qkT[h][:, 0, tok], Sb[:, HS[h]])
                        p.mm(psAq[:, HS[h]], qkT[h][:, 1, tok], Sb[:, HS[h]])
                    Rm = p.tmp(sc_, "gd_Rm", [128, 512], BF16, 1)
                    qSe = p.tmp(sc_, "gd_qSe", [128, 512], F32, 1)
                    vnew = p.tmp(sc_, "gd_vnew", [128, 512], BF16, 1)
                    for h in H4:
                        p.stt(Rm[:, HS[h]], psAk[:, HS[h]], negegcA[:, c, h:h + 1], vtm[:, HS[h]], ALU.mult, ALU.add)
                    p.tt("dve", h4(qSe[:]), h4(psAq[:]), bce(egA[:, c * 8:c * 8 + 4]), ALU.mult)
                    yield
                    psV = p.ps()
                    for h in H4:
                        p.mm(psV[:, HS[h]], TpT[h], Rm[:, HS[h]])
                    p.tt("dve", h4(vnew[:]), h4(psV[:]), bce(betaA[:, c, :]), ALU.mult)
                    yield
                    oall = p.tmp(sc_, "gd_oall", [128, 512], F32, 1)
                    psO = p.ps()
                    psS = p.ps()
                    for h in H4:
                        p.mm(psO[:, HS[h]], aqkT[:, HS[h]], vnew[:, HS[h]])
                        p.mm(psS[:, HS[h]], kdec[:, HS[h]], vnew[:, HS[h]])
                    p.tt("dve", h4(Sst[:]), h4(Sst[:]), bce(egA[:, c * 8 + 4:c * 8 + 8]), ALU.mult)
                    p.tt("dve", Sst[:], Sst[:], psS[:], ALU.add)
                    p.tt("dve", oall[:], psO[:], qSe[:], ALU.add)
                    p.act(Sb[:], Sst[:], AF.Copy)
                    yield
                    p.tag = f"gd_out:{c}"
                    sm = p.tmp(sc_, "gd_sm", [128, 64], F32, 2)
                    osq = p.tmp(sc_, "gd_osq", [128, 512], BF16, 1)
                    p.act(osq[:], oall[:], AF.Square)
                    ssq = sm[:, 48:52]
                    p.op("dve", lambda e, ssq=ssq, osq=osq: e.tensor_reduce(
                        ssq, osq[:].rearrange("p (h d) -> p h d", h=4), AX.X, ALU.add), [osq], [ssq])
                    yield
                    lnv2 = sm[:, 52:56]
                    p.act(lnv2, ssq, AF.Ln, bias=EPS, scale=1.0 / 128)
                    p.act(sm[:, 56:60], lnv2, AF.Exp, scale=-0.5)
                    ytm = p.tmp(sc_, "gd_ytm", [128, 512], BF16, 1)
                    for h in H4:
                        p.stt(ytm[:, HS[h]], oall[:, HS[h]], sm[:, 56 + h:57 + h], sz[:, HS[h]], ALU.mult, ALU.mult)
                    yield
                    transpose_out(sc_, ytm, yT, tok, [pp_[:, PP_GDNW:PP_GDNW + 1]])
                    yield

                def run(g):
                    for _ in g:
                        pass

                def weave(g1, g2):
                    a = b = True
                    while a or b:
                        for _ in range(2):
                            if a:
                                try:
                                    next(g1)
                                except StopIteration:
                                    a = False
                        if b:
                            try:
                                next(g2)
                            except StopIteration:
                                b = False

                for t in range(NT):
                    p.tag = f"gd_feat:{t}"
                    with ExitStack() as sf:
                        def stage_a(pj, h):
                            Wm = (Wq, Wk, Wv)[pj]
                            hs = slice(h * 128, (h + 1) * 128)
                            psx = p.ps()
                            for kc in range(8):
                                p.mm(psx[:], Wm[:, kc, hs], hn[kc][t][:], start=(kc == 0), stop=(kc == 7))
                            i = pj * 4 + h
                            xc = conv4(sf, psx, tails[i],
                                       lambda k, i=i: pp_[:, PP_GCW + i * 4 + k:PP_GCW + i * 4 + k + 1], None, "gd", 4)
                            return (pj, h, xc, conv4.xe)

                        def stage_b(us):
                            for (pj, h, xc, xe) in us:
                                if pj == 2:
                                    p.act(vT[h][:], xc[:], AF.Silu)
                                else:
                                    p.act(xc[:], xc[:], AF.Silu)
                            pend = []
                            for (pj, h, xc, xe) in us:
                                if pj == 2:
                                    continue
                                sqb = p.tmp(sf, "gd_sqb", [128, 512], BF16, 2)
                                p.act(sqb[:], xc[:], AF.Square)
                                pss = p.ps()
                                p.mm(pss[:], cb(C_ONES), sqb[:])
                                pend.append((pj, h, xc, xe, pss))
                            for (pj, h, xc, xe, pss) in pend:
                                p.act(xe[:, 0:512], pss[:], AF.Ln, bias=EPS)
                                bias = -0.5 * math.log(128.0) if pj == 0 else 0.0
                                p.act(xe[:, 0:512], xe[:, 0:512], AF.Exp, scale=-0.5, bias=bias)
                            for (pj, h, xc, xe, pss) in pend:
                                p.tt("dve", qkT[h][:, 1 - pj, :], xc[:], xe[:, 0:512], ALU.mult)
                        units = [(pj, h) for pj in range(3) for h in range(4)]

                        def feat_gen():
                            prev = None
                            for i2 in range(0, 12, 2):
                                cur = [stage_a(*units[i2])]
                                yield
                                cur.append(stage_a(*units[i2 + 1]))
                                yield
                                if prev is not None:
                                    stage_b(prev)
                                    yield
                                prev = cur
                            stage_b(prev)
                            yield
                        if t == 0:
                            run_(feat_gen())
                        else:
                            p.tag = f"gd_feat:{t}"
                            weave_(feat_gen(), out_proj_g(Wout, yT, t - 1, gd_tmpo))
                        p.barrier()
                    with ExitStack() as sc_:
                        us = [dict() for _ in range(4)]
                        szs = gd_zburst(sc_, t)
                        run(gd_part1(sc_, t, 0, us[0], szs[0]))
                        for cc in range(4):
                            if cc + 1 < 4:
                                weave(gd_part1(sc_, t, cc + 1, us[cc + 1], szs[cc + 1]), gd_part2(sc_, us[cc], yT))
                            else:
                                run(gd_part2(sc_, us[cc], yT))
                        p.barrier()
                p.tag = "gd_oproj:3"
                run_(out_proj_g(Wout, yT, NT - 1, gd_tmpo))
                p.barrier()

        groups = []
        for l in range(n_layers):
            if "rg" in phases:
                groups += [(l, "in", COLS["rg_x"]), (l, "in", COLS["rg_z"]), (l, "out", 0)]
            if "ml" in phases:
                groups += [(l, "in", COLS[k]) for k in ("ml_q", "ml_k", "ml_v", "ml_o", "ml_z")] + [(l, "out", 1)]
            if "gd" in phases:
                groups += [(l, "in", COLS[k]) for k in ("gd_q", "gd_k", "gd_v", "gd_z")] + [(l, "out", 2)]
        gpos = [0]
        loaded = []

        def prefetch(upto):
            while gpos[0] < min(upto, len(groups)):
                loaded.append(load_group(*groups[gpos[0]]))
                gpos[0] += 1

        def take(n):
            prefetch(len(loaded_used) + n)
            r = loaded[len(loaded_used):len(loaded_used) + n]
            loaded_used.extend(r)
            return r
        loaded_used = []

        load_small(0)
        prefetch(NSLOT)
        for l in range(n_layers):
            p.epoch = l + 1
            if l + 1 < n_layers:
                load_small(l + 1)
            load_small2(l)
            with ExitStack() as st:
                pp_ = ppt[l % 2]
                rms_to(lambda c, t: hn[c][t][:], None, lambda c: pp_[:, PP_NW + c:PP_NW + c + 1], st)
                p.barrier()
            if "rg" in phases:
                W = take(3)
                rg_phase(l, W)
                prefetch(len(loaded_used) + NSLOT)
            if "ml" in phases:
                W = take(6)
                ml_phase(l, W)
                prefetch(len(loaded_used) + (5 if "gd" in phases else NSLOT))
            if "gd" in phases:
                W = take(5)
                scratch = ring[ring_pos[0] % NSLOT]
                gd_phase(l, W, scratch)
                prefetch(len(loaded_used) + NSLOT)

        finals = []
        with ExitStack() as st:
            if final_norm:
                pp_ = ppt[(n_layers - 1) % 2]
                outb = {}

                def dst(c, t):
                    o = p.tmp(st, "fin_o", [128, 512], F32, 4)
                    outb[(c, t)] = o
                    return o[:]
                for t in range(NT):
                    ss = p.ps()
                    for c in range(8):
                        sq = p.tmp(st, "n_sq", [128, 512], BF16, 3)
                        p.act(sq[:], xT[c][t][:], AF.Square)
                        p.mm(ss[:], cb(C_ONES), sq[:], start=(c == 0), stop=(c == 7))
                    lnv = p.tmp(st, "n_ln", [128, 512], F32, 2)
                    p.act(lnv[:], ss[:], AF.Ln, bias=EPS, scale=1.0 / D)
                    rstd = p.tmp(st, "n_rs", [128, 512], F32, 2)
                    p.act(rstd[:], lnv[:], AF.Exp, scale=-0.5)
                    for c in range(8):
                        o = p.tmp(st, "fin_o", [128, 512], F32, 4)
                        p.stt(o[:], xT[c][t][:], pp_[:, PP_FNW + c:PP_FNW + c + 1], rstd[:], ALU.mult, ALU.mult)
                        tk = p.dma("sp", f"o{(c + t * 8) % 4}", y_d[c, :, t * 512:(t + 1) * 512], o[:])
                        finals.append(tk)
            else:
                for t in range(NT):
                    for c in range(8):
                        tk = p.dma("sp", f"o{(c + t * 8) % 4}", y_d[c, :, t * 512:(t + 1) * 512], xT[c][t][:])
                        finals.append(tk)
            fw = {}
            for tk in finals:
                fw[tk[:3]] = max(fw.get(tk[:3], 0), tk[3])
            p.finalize(list(fw.items()))
    return nc, p


_CACHE = {}


def _get(n_layers, final_norm):
    k = (n_layers, final_norm)
    if k not in _CACHE:
        _CACHE[k] = build(n_layers, final_norm)[0]
    return _CACHE[k]


def kernel(**inp):
    inp = {k: np.asarray(v) for k, v in inp.items()}
    x = inp["x"].astype(np.float32)
    B = x.shape[0]
    pp, pb, gw, wg = pack_params(inp)
    cst = make_consts()
    nc = _get(DEPTH, True)
    w_in = np.ascontiguousarray(inp["w_in"], dtype=np.float32)
    w_out = np.ascontiguousarray(inp["w_out"], dtype=np.float32)
    in_maps = []
    for b in range(B):
        xt = np.ascontiguousarray(x[b].T.reshape(8, 128, S))
        in_maps.append(dict(x=xt, w_in=w_in, w_out=w_out, pp=pp, pb=pb, gw=gw, wg=wg, cst=cst))
    res = run_bass_kernel_spmd(nc, in_maps, core_ids=list(range(B)))
    out = np.stack([np.asarray(r["y"]).reshape(D, S).T for r in res.results], axis=0)
    return np.ascontiguousarray(out.astype(np.float32))
```

```python
import math
from contextlib import ExitStack
import numpy as np
import concourse.bass as bass
import concourse.mybir as mybir
from concourse.bass_utils import run_bass_kernel_spmd

F32 = mybir.dt.float32
BF16 = mybir.dt.bfloat16
AF = mybir.ActivationFunctionType
ALU = mybir.AluOpType
AX = mybir.AxisListType

S = 2048
D = 1024
DEPTH = 4
NT = 4
EPS = 1e-6
NSLOT = 6
ENGS = ("pe", "act", "dve", "pool", "sp")


class V:
    def __init__(self, ap, key):
        self.ap = ap
        self.key = key

    def __getitem__(self, idx):
        return V(self.ap[idx], self.key)


def U(x):
    return x.ap if isinstance(x, V) else x


def KN(x):
    return x.key if isinstance(x, V) else x.name


class Prog:
    def __init__(self, nc):
        self.nc = nc
        self.stack = ExitStack()
        self.streams = {e: [] for e in ENGS}
        self.epoch = 0
        self.cnt = {}
        self.know = {e: {} for e in ENGS}
        self.last_w = {}
        self.readers = {}
        self.kn_at = {}
        self.dma_tot = {}
        self.ps_banks = []
        self.ps_i = 0
        self.tmps = {}
        self.nops = 0
        self.ps_reader = {}
        self.tags = {}
        self.tag = ""

    def sb(self, name, shape, dtype, stack=None):
        st = stack or self.stack
        self.uid = getattr(self, "uid", 0) + 1
        return st.enter_context(self.nc.sbuf_tensor(f"{name}_u{self.uid}", list(shape), dtype))

    def sb_once(self, name, shape, dtype):
        if not hasattr(self, "_once"):
            self._once = {}
        if name not in self._once:
            self._once[name] = self.sb(name, shape, dtype)
        return self._once[name]

    def init_psum(self):
        for i in range(8):
            self.ps_banks.append(self.stack.enter_context(self.nc.psum_tensor(f"psb{i}", [128, 512], F32)))

    def ps(self):
        t = self.ps_banks[self.ps_i % 8]
        self.ps_i += 1
        return t

    def tmp(self, stack, name, shape, dtype, bufs=2):
        key = name
        if key not in self.tmps or self.tmps[key][0] is not stack:
            self.tmps[key] = [stack, [self.sb(f"{name}_{i}", shape, dtype, stack) for i in range(bufs)], 0]
        ent = self.tmps[key]
        t = ent[1][ent[2] % len(ent[1])]
        ent[2] += 1
        return t

    def _deps(self, eng, reads, writes):
        deps = []
        for r in reads:
            if r.startswith("psb"):
                t = self.ps_reader.get(r)
                if t is not None and t[1] != eng:
                    deps.append(t)
        for r in reads:
            t = self.last_w.get(r)
            if t is not None:
                deps.append(t)
        for w in writes:
            t = self.last_w.get(w)
            if t is not None and not (t[0] == "eng" and t[1] == eng):
                deps.append(t)
            for t in self.readers.get(w, ()):
                if not (t[0] == "eng" and t[1] == eng):
                    deps.append(t)
        waits = []
        kn = self.know[eng]
        for t in deps:
            key, val = t[:3], t[3]
            if key[0] == "eng" and key[1] == eng and eng == "pe":
                continue
            if kn.get(key, 0) >= val:
                continue
            waits.append((key, val))
            for k2, v2 in self.kn_at[t].items():
                if kn.get(k2, 0) < v2:
                    kn[k2] = v2
            kn[key] = max(kn.get(key, 0), val)
        best = {}
        for key, val in waits:
            best[key] = max(best.get(key, 0), val)
        return list(best.items())

    def _commit(self, ticket, eng, reads, writes):
        snap = dict(self.know[eng])
        self.kn_at[ticket] = snap
        for r in reads:
            if r in writes:
                continue
            self.readers.setdefault(r, []).append(ticket)
        for w in writes:
            self.last_w[w] = ticket
            self.readers[w] = []

    def op(self, eng, fn, reads, writes):
        reads = [KN(a) for a in reads]
        writes = [KN(a) for a in writes]
        waits = self._deps(eng, reads, writes)
        k = (eng, self.epoch)
        self.cnt[k] = self.cnt.get(k, 0) + 1
        ticket = ("eng", eng, self.epoch, self.cnt[k])
        self.streams[eng].append((waits, fn, ("eng", eng, self.epoch), 1))
        self.tags.setdefault(eng, []).append(self.tag)
        self._commit(ticket, eng, reads, writes)
        for r in reads:
            if r.startswith("psb"):
                self.ps_reader[r] = ticket
        for w in writes:
            if w.startswith("psb"):
                self.ps_reader.pop(w, None)
        self.nops += 1
        return ticket

    def dma(self, eng, semkey, out, in_, extra_reads=()):
        reads = [in_.name] + [a.name for a in extra_reads]
        writes = [out.name]
        waits = self._deps(eng, reads, writes)
        prev = self.dma_tot.get(semkey, 0)
        key = ("dma", semkey, 0)
        if prev and self.know[eng].get(key, 0) < prev:
            waits.append((key, prev))
            self.know[eng][key] = prev
        tot = prev + 16
        self.dma_tot[semkey] = tot
        ticket = ("dma", semkey, 0, tot)

        def fn(e, out=out, in_=in_):
            return e.dma_start(out=out, in_=in_)
        self.streams[eng].append((waits, fn, key, 16))
        self._commit(ticket, eng, reads, writes)
        return ticket

    def barrier(self):
        latest = []
        for (e, ep), c in self.cnt.items():
            latest.append((("eng", e, ep), c))
        for e in ENGS:
            waits = []
            for key, val in latest:
                if key[1] == e and e == "pe":
                    continue
                if self.know[e].get(key, 0) < val:
                    waits.append((key, val))
                    self.know[e][key] = val
            if waits:
                self.streams[e].append((waits, None, None, 0))

    def finalize(self, final_waits):
        nc = self.nc
        sems = {}
        keys = set()
        for e in ENGS:
            for waits, fn, inc, n in self.streams[e]:
                for k, v in waits:
                    keys.add(k)
                if inc is not None:
                    keys.add(inc)
        for k in sorted(keys, key=str):
            sems[k] = self.stack.enter_context(nc.semaphore("s_" + "_".join(str(x) for x in k)))
        streams = self.streams

        def replay(e, eng):
            for waits, fn, inc, n in streams[e]:
                for k, v in waits:
                    eng.wait_ge(sems[k], v)
                if fn is not None:
                    ins = fn(eng)
                    ins.then_inc(sems[inc], n)
            if e == "sp":
                for k, v in final_waits:
                    eng.wait_ge(sems[k], v)

        with nc.Block() as block:
            @block.tensor
            def _(eng):
                replay("pe", eng)

            @block.scalar
            def _(eng):
                replay("act", eng)

            @block.vector
            def _(eng):
                replay("dve", eng)

            @block.gpsimd
            def _(eng):
                replay("pool", eng)

            @block.sync
            def _(eng):
                replay("sp", eng)

    def mm(self, out, lhsT, rhs, start=True, stop=True):
        o_, l_, r_ = U(out), U(lhsT), U(rhs)
        return self.op("pe", lambda e: e.matmul(o_, l_, r_, start=start, stop=stop), [lhsT, rhs], [out])

    def tr(self, out, in_, ident):
        o_, i_ = U(out), U(in_)
        return self.op("pe", lambda e: e.transpose(o_, i_, ident), [in_, ident], [out])

    def act(self, out, in_, func, bias=None, scale=None):
        kw = {}
        rd = [in_]
        if bias is not None:
            kw["bias"] = bias
            if not isinstance(bias, (int, float)):
                rd.append(bias)
        if scale is not None:
            kw["scale"] = scale
            if not isinstance(scale, (int, float)):
                rd.append(scale)
        o_, i_ = U(out), U(in_)
        return self.op("act", lambda e: e.activation(o_, i_, func, **kw), rd, [out])

    def tt(self, eng, out, in0, in1, op):
        o_, a_, b_ = U(out), U(in0), U(in1)
        return self.op(eng, lambda e: e.tensor_tensor(o_, a_, b_, op), [in0, in1], [out])

    def ts(self, eng, out, in0, s1, op0, s2=None, op1=None):
        rd = [in0] + [s for s in (s1, s2) if s is not None and not isinstance(s, (int, float))]
        if op1 is None:
            return self.op(eng, lambda e: e.tensor_scalar(out, in0, s1, None, op0), rd, [out])
        return self.op(eng, lambda e: e.tensor_scalar(out, in0, s1, s2, op0, op1), rd, [out])

    def stt(self, out, in0, scalar, in1, op0, op1):
        rd = [in0, in1] + ([] if isinstance(scalar, (int, float)) else [scalar])
        o_, a_, b_ = U(out), U(in0), U(in1)
        return self.op("dve", lambda e: e.scalar_tensor_tensor(o_, a_, scalar, b_, op0, op1), rd, [out])

    def copy(self, eng, out, in_):
        if eng == "act":
            return self.act(out, in_, AF.Copy)
        o_, i_ = U(out), U(in_)
        return self.op(eng, lambda e: e.tensor_copy(o_, i_), [in_], [out])

    def memset(self, eng, ap, val):
        return self.op(eng, lambda e: e.memset(ap, val), [], [ap])


L = 128
C_ID, C_ONES, C_U, C_SU, C_NBD, C_CU0, C_CU1, C_CU2, C_CL0, C_CL1, C_POS = range(11)
NCONST = 11


def _blockmask(bs):
    m = np.zeros((L, L), np.float32)
    for i in range(L // bs):
        m[i * bs:(i + 1) * bs, i * bs:(i + 1) * bs] = 1
    return m


def make_consts():
    U = np.triu(np.ones((L, L), np.float32))
    SU = np.triu(np.ones((L, L), np.float32), 1)
    bd = [_blockmask(b) for b in (16, 32, 64, 128)]
    c = np.zeros((NCONST, L, L), np.float32)
    c[C_ID] = np.eye(L)
    c[C_ONES] = 1.0
    c[C_U] = U
    c[C_SU] = SU
    c[C_NBD] = -bd[0] * SU
    for lv in range(3):
        c[C_CU0 + lv] = (bd[lv + 1] - bd[lv]) * SU
    for lv in range(2):
        c[C_CL0 + lv] = (bd[lv + 1] - bd[lv]) * SU.T
    c[C_POS] = 30000.0 * (1.0 - U)
    return np.ascontiguousarray(c.transpose(1, 0, 2).reshape(L, NCONST * L))


PP_NW = 0
PP_RCW = 8
PP_RCB = 24
PP_RGB = 28
PP_LAM = 36
PP_GCW = 40
PP_FNW = 88
PP_MLNW = 96
PP_GDNW = 100
NPP = 104
PB_MLB = 0
PB_DTB = 128
PB_ALOG = 192
NPB = 256


def pack_params(inp):
    pp = np.zeros((DEPTH, 128, NPP), np.float32)
    pb = np.zeros((DEPTH, 128, NPB), np.float32)
    gw = np.zeros((DEPTH, 8, 128, 128), np.float32)
    wg = np.zeros((DEPTH, D, 16), np.float32)
    for l in range(DEPTH):
        pp[l, :, PP_NW:PP_NW + 8] = inp["norm_w"][l].reshape(8, 128).T
        pp[l, :, PP_RCW:PP_RCW + 16] = inp["rg_conv_w"][l].reshape(4, 4, 128).transpose(2, 1, 0).reshape(128, 16)
        pp[l, :, PP_RCB:PP_RCB + 4] = inp["rg_conv_b"][l].reshape(4, 128).T
        pp[l, :, PP_RGB:PP_RGB + 8] = inp["rg_gate_b"][l].reshape(2, 4, 128).transpose(2, 0, 1).reshape(128, 8)
        pp[l, :, PP_LAM:PP_LAM + 4] = inp["rg_lambda"][l].reshape(4, 128).T
        pp[l, :, PP_GCW:PP_GCW + 48] = inp["gd_conv_w"][l].reshape(4, 12, 128).transpose(2, 1, 0).reshape(128, 48)
        pp[l, :, PP_FNW:PP_FNW + 8] = inp["final_norm_w"].reshape(8, 128).T
        pp[l, :, PP_MLNW:PP_MLNW + 4] = inp["ml_norm_w"][l].reshape(4, 128).T
        pp[l, :, PP_GDNW] = inp["gd_norm_w"][l]
        pb[l, :, PB_MLB:PB_MLB + 128] = np.tile(inp["ml_gate_b"][l].reshape(8), 16)[None]
        pb[l, :, PB_DTB:PB_DTB + 64] = np.tile(inp["gd_dt_bias"][l], 16)[None]
        pb[l, :, PB_ALOG:PB_ALOG + 64] = np.tile(inp["gd_a_log"][l], 16)[None]
        for g in range(2):
            for c in range(4):
                for h2 in range(2):
                    n = c * 2 + h2
                    gw[l, g * 4 + c, h2 * 64:(h2 + 1) * 64, h2 * 64:(h2 + 1) * 64] = inp["rg_gate_w"][l, g, n]
        wg[l, :, 0:8] = inp["w_in"][l][:, 3584:3592]
        wg[l, :, 8:16] = inp["w_in"][l][:, 5640:5648]
    return pp, pb, gw, wg


COLS = dict(rg_x=0, rg_z=512, ml_q=1024, ml_k=1536, ml_v=2048, ml_o=2560, ml_z=3072,
            gd_q=3592, gd_k=4104, gd_v=4616, gd_z=5128)


def build(n_layers, final_norm, phases=("rg", "ml", "gd")):
    nc = bass.Bass("TRN2", target_bir_lowering=False)
    x_d = nc.dram_tensor("x", [8, 128, S], F32, kind="ExternalInput").ap()
    win_d = nc.dram_tensor("w_in", [n_layers, D, 5648], F32, kind="ExternalInput").ap()
    wout_d = nc.dram_tensor("w_out", [n_layers, 1536, D], F32, kind="ExternalInput").ap()
    pp_d = nc.dram_tensor("pp", [n_layers, 128, NPP], F32, kind="ExternalInput").ap()
    pb_d = nc.dram_tensor("pb", [n_layers, 128, NPB], F32, kind="ExternalInput").ap()
    gw_d = nc.dram_tensor("gw", [n_layers, 8, 128, 128], F32, kind="ExternalInput").ap()
    wg_d = nc.dram_tensor("wg", [n_layers, D, 16], F32, kind="ExternalInput").ap()
    cst_d = nc.dram_tensor("cst", [128, NCONST * 128], F32, kind="ExternalInput").ap()
    y_d = nc.dram_tensor("y", [8, 128, S], F32, kind="ExternalOutput").ap()

    p = Prog(nc)
    with p.stack:
        p.init_psum()
        xT = [[p.sb(f"xT{c}_{t}", [128, 512], F32) for t in range(NT)] for c in range(8)]
        hn = [[p.sb(f"hn{c}_{t}", [128, 512], BF16) for t in range(NT)] for c in range(8)]
        ring = [p.sb(f"ring{i}", [128, 4096], BF16) for i in range(NSLOT)]
        FSET = [C_ID, C_ONES, C_U, C_SU, C_POS]
        BSET = [C_ID, C_ONES, C_NBD, C_CU0, C_CU1, C_CU2, C_CL0, C_CL1]
        cstf = p.sb("cstf", [128, len(FSET) * 128], F32)
        cstb = p.sb("cstb", [128, len(BSET) * 128], BF16)
        ii_b = p.sb("ii_b", [128, 256], BF16)
        ppt = [p.sb(f"pp{i}", [128, NPP], F32) for i in range(2)]
        pbt = [p.sb("pb0", [128, NPB], F32)] * 2
        gwt = [p.sb("gw0", [128, 8, 128], BF16)] * 2
        wgt = [p.sb("wg0", [128, 8, 16], BF16)] * 2

        def cf(i):
            j = FSET.index(i)
            return cstf[:, j * 128:(j + 1) * 128]

        def cb(i):
            j = BSET.index(i)
            return cstb[:, j * 128:(j + 1) * 128]

        with ExitStack() as st0:
            cst = p.sb("cst_stage", [128, NCONST * 128], F32, st0)
            p.dma("sp", "cst", cst[:], cst_d)
            for j, i in enumerate(FSET):
                p.copy("dve", cstf[:, j * 128:(j + 1) * 128], cst[:, i * 128:(i + 1) * 128])
            for j, i in enumerate(BSET):
                p.copy("dve", cstb[:, j * 128:(j + 1) * 128], cst[:, i * 128:(i + 1) * 128])
            p.copy("pool", ii_b[:, 0:128], cst[:, C_ID * 128:(C_ID + 1) * 128])
            p.copy("pool", ii_b[:, 128:256], cst[:, C_ID * 128:(C_ID + 1) * 128])
            p.barrier()
        for t in range(NT):
            for c in range(8):
                p.dma("sp", f"x{(c * NT + t) % 4}", xT[c][t][:], x_d[c, :, t * 512:(t + 1) * 512])

        ring_pos = [0]

        def load_group(l, kind, idx):
            slot = ring[ring_pos[0] % NSLOT]
            sk = f"ring{ring_pos[0] % NSLOT}"
            ring_pos[0] += 1
            if kind == "in":
                v = slot[:].rearrange("p (k c) -> p k c", k=8)
                src = win_d[l, :, idx:idx + 512].rearrange("(k p) c -> p k c", p=128)
                for half in range(2):
                    p.dma("pool", sk + f"_{half}", v[:, half * 4:(half + 1) * 4, :], src[:, half * 4:(half + 1) * 4, :])
                return v
            else:
                v = slot[:].rearrange("p (j m) -> p j m", j=4)
                src = wout_d[l, idx * 512:(idx + 1) * 512, :].rearrange("(j p) m -> p j m", p=128)
                for half in range(2):
                    p.dma("pool", sk + f"_{half}", v[:, half * 2:(half + 1) * 2, :], src[:, half * 2:(half + 1) * 2, :])
                return v

        def load_small(l):
            i = l % 2
            p.dma("sp", "pp", ppt[i][:], pp_d[l])

        def load_small2(l):
            p.dma("sp", "pb", pbt[0][:], pb_d[l])
            p.dma("pool", "gw", gwt[0][:], gw_d[l].rearrange("g p m -> p g m"))
            p.dma("pool", "wg", wgt[0][:], wg_d[l].rearrange("(k p) c -> p k c", p=128))

        def rms_to(dst, src_list_fn, pw_ap_fn, stack, fp32_out=False):
            for t in range(NT):
                ss = p.ps()
                for c in range(8):
                    sq = p.tmp(stack, "n_sq", [128, 512], BF16, 3)
                    p.act(sq[:], xT[c][t][:], AF.Square)
                    p.mm(ss[:], cb(C_ONES), sq[:], start=(c == 0), stop=(c == 7))
                lnv = p.tmp(stack, "n_ln", [128, 512], F32, 2)
                p.act(lnv[:], ss[:], AF.Ln, bias=EPS, scale=1.0 / D)
                rstd = p.tmp(stack, "n_rs", [128, 512], F32, 2)
                p.act(rstd[:], lnv[:], AF.Exp, scale=-0.5)
                for c in range(8):
                    p.stt(dst(c, t), xT[c][t][:], pw_ap_fn(c), rstd[:], ALU.mult, ALU.mult)

        def out_proj(Wo, yT, t, stk):
            for m in range(8):
                pso = p.ps()
                for j in range(4):
                    p.mm(pso[:], Wo[:, j, m * 128:(m + 1) * 128], yT[:, j, :], start=(j == 0), stop=(j == 3))
                tmpo = p.tmp(stk, "op_tmp", [128, 512], F32, 1)
                p.act(tmpo[:], pso[:], AF.Copy)
                p.tt("dve", xT[m][t][:], tmpo[:], xT[m][t][:], ALU.add)

        def out_proj_g(Wo, yT, t, tmpo):
            for m in range(8):
                pso = p.ps()
                for j in range(4):
                    p.mm(pso[:], Wo[:, j, m * 128:(m + 1) * 128], yT[:, j, :], start=(j == 0), stop=(j == 3))
                p.act(tmpo[:], pso[:], AF.Copy)
                p.tt("dve", xT[m][t][:], tmpo[:], xT[m][t][:], ALU.add)
                yield

        def chain_(*gs):
            for g in gs:
                yield from g

        def conv4(stack, psx, tail, wcol, bias_ap, name, bufs=1, xe_bufs=None):
            xe = p.tmp(stack, name + "_xe", [128, 515], F32, xe_bufs or bufs)
            p.copy("pool", xe[:, 0:3], tail[:])
            p.act(xe[:, 3:515], psx[:], AF.Copy)
            p.copy("pool", tail[:], xe[:, 512:515])
            xc = p.tmp(stack, name + "_xc", [128, 512], F32, bufs)
            if bias_ap is not None:
                p.ts("dve", xc[:], xe[:, 3:515], wcol(3), ALU.mult, bias_ap, ALU.add)
            else:
                p.act(xc[:], psx[:], AF.Copy, scale=wcol(3))
            for k in range(3):
                p.stt(xc[:], xe[:, k:k + 512], wcol(k), xc[:], ALU.mult, ALU.add)
            conv4.xe = xe
            return xc

        def transpose_out(stack, y_tm, yT, tok, nw_cols):
            pst = p.ps()
            pv = pst[:].bitcast(BF16)
            for h in range(4):
                p.tr(pv[:, h * 128:(h + 1) * 128], y_tm[:, h * 128:(h + 1) * 128], cb(C_ID))
            if len(nw_cols) == 1:
                p.act(yT[:, :, tok], pv[:, 0:512].rearrange("p (h t) -> p h t", h=4), AF.Copy, scale=nw_cols[0])
            else:
                for h in range(4):
                    p.act(yT[:, h, tok], pv[:, h * 128:(h + 1) * 128], AF.Copy, scale=nw_cols[h])

        def h4(ap):
            return ap.rearrange("p (h d) -> p h d", h=4)

        def bce(ap4):
            return ap4.to_broadcast([128, 4, 128])

        def bch(ap128):
            return ap128.unsqueeze(1).to_broadcast([128, 4, 128])

        def run_(g):
            for _ in g:
                pass

        def weave_(g1, g2):
            a = b = True
            while a or b:
                if a:
                    try:
                        next(g1)
                    except StopIteration:
                        a = False
                if b:
                    try:
                        next(g2)
                    except StopIteration:
                        b = False

        def rg_phase(l, W):
            pp_ = ppt[l % 2]
            gw_ = gwt[l % 2]
            Wx, Wz, Wo = W
            with ExitStack() as st:
                tail = [p.sb(f"rg_tail{j}", [128, 3], F32, st) for j in range(4)]
                hprev = [p.sb(f"rg_hp{j}", [128, 1], F32, st) for j in range(4)]
                sm = p.sb("rg_sm", [128, 32], F32, st)
                for j in range(4):
                    p.memset("pool", tail[j][:], 0.0)
                    p.memset("pool", hprev[j][:], 0.0)
                lam = pp_[:, PP_LAM:PP_LAM + 4]
                yv, uv, lv, dv, rv, cfv, cf2v = (sm[:, 4 * i:4 * i + 4] for i in range(7))
                p.act(yv, lam, AF.Exp, scale=-1.0)
                p.ts("pool", uv, yv, 1.0, ALU.add)
                p.act(lv, uv, AF.Ln)
                p.ts("pool", dv, uv, -1.0, ALU.add, 1e-30, ALU.max)
                p.op("dve", lambda e: e.reciprocal(rv, dv), [dv], [rv])
                p.tt("pool", lv, lv, yv, ALU.mult)
                p.tt("pool", lv, lv, rv, ALU.mult)
                p.ts("pool", cfv, lv, -8.0, ALU.mult)
                p.ts("pool", cf2v, lv, -16.0, ALU.mult)
                yrgs = [p.sb(f"rg_y{i}", [128, 4, 512], BF16, st) for i in range(2)]
                rg_tmpo = p.sb("rg_tmpo", [128, 512], F32, st)

                def gen_a(t, jp, out):
                    us = []
                    pss = []
                    for j in (2 * jp, 2 * jp + 1):
                        js = slice(j * 128, (j + 1) * 128)
                        psx = p.ps()
                        for kc in range(8):
                            p.mm(psx[:], Wx[:, kc, js], hn[kc][t][:], start=(kc == 0), stop=(kc == 7))
                        pss.append(psx)
                    for j in (2 * jp, 2 * jp + 1):
                        js = slice(j * 128, (j + 1) * 128)
                        psz = p.ps()
                        for kc in range(8):
                            p.mm(psz[:], Wz[:, kc, js], hn[kc][t][:], start=(kc == 0), stop=(kc == 7))
                        pss.append(psz)
                    for i_, j in enumerate((2 * jp, 2 * jp + 1)):
                        xc = conv4(st, pss[i_], tail[j], lambda k, j=j: pp_[:, PP_RCW + j * 4 + k:PP_RCW + j * 4 + k + 1],
                                   pp_[:, PP_RCB + j:PP_RCB + j + 1], "rg", 4, 2)
                        us.append(dict(t=t, j=j, xc=xc))
                    for i_, u in enumerate(us):
                        u["sz"] = p.tmp(st, "rg_szb", [128, 512], BF16, 4)
                        p.act(u["sz"][:], pss[2 + i_][:], AF.Silu)
                    yield
                    for u in us:
                        j = u["j"]
                        xcb = p.tmp(st, "rg_xcb", [128, 512], BF16, 2)
                        p.copy("dve", xcb[:], u["xc"][:])
                        psr = p.ps()
                        p.mm(psr[:], gw_[:, j, :], xcb[:])
                        psi = p.ps()
                        p.mm(psi[:], gw_[:, 4 + j, :], xcb[:])
                        u["r"] = p.tmp(st, "rg_r", [128, 512], F32, 4)
                        p.act(u["r"][:], psr[:], AF.Sigmoid, bias=pp_[:, PP_RGB + j:PP_RGB + j + 1])
                        u["i"] = p.tmp(st, "rg_i", [128, 512], F32, 4)
                        p.act(u["i"][:], psi[:], AF.Sigmoid, bias=pp_[:, PP_RGB + 4 + j:PP_RGB + 4 + j + 1])
                        yield
                    out.extend(us)

                def gen_b(us):
                    for u in us:
                        j = u["j"]
                        u["a"] = p.tmp(st, "rg_a", [128, 512], F32, 2)
                        p.act(u["a"][:], u["r"][:], AF.Exp, scale=sm[:, 20 + j:21 + j])
                        u["a2"] = u["r"]
                        p.act(u["a2"][:], u["r"][:], AF.Exp, scale=sm[:, 24 + j:25 + j])
                        p.tt("dve", u["i"][:], u["i"][:], u["xc"][:], ALU.mult)
                    yield
                    for u in us:
                        p.act(u["a2"][:], u["a2"][:], AF.Sqrt, bias=1.0, scale=-1.0)
                        p.tt("pool", u["i"][:], u["i"][:], u["a2"][:], ALU.mult)
                    yield
                    for u in us:
                        j = u["j"]
                        hh = p.tmp(st, "rg_h", [128, 512], F32, 2)
                        u["h"] = hh
                        hp = hprev[j]
                        a = u["a"]
                        ig = u["i"]
                        p.op("dve", lambda e, hh=hh, a=a, ig=ig, hp=hp: e.tensor_tensor_scan(
                            hh[:], a[:], ig[:], hp[:], ALU.mult, ALU.add), [a, ig, hp], [hh])
                        p.copy("pool", hp[:], hh[:, 511:512])
                    yield
                    for u in us:
                        p.tt("dve", yrgs[u["t"] % 2][:, u["j"], :], u["h"][:], u["sz"][:], ALU.mult)
                    yield

                pairs = [(t, jp) for t in range(NT) for jp in range(2)]
                prev = None
                for (t, jp) in pairs:
                    p.tag = f"rg:{t}"
                    cur = []
                    if prev is None:
                        run_(gen_a(t, jp, cur))
                    elif prev[0]["j"] == 2:
                        weave_(gen_a(t, jp, cur),
                               chain_(gen_b(prev), out_proj_g(Wo, yrgs[prev[0]["t"] % 2], prev[0]["t"], rg_tmpo)))
                    else:
                        weave_(gen_a(t, jp, cur), gen_b(prev))
                    prev = cur
                run_(gen_b(prev))
                out_proj(Wo, yrgs[prev[0]["t"] % 2], prev[0]["t"], st)
                p.barrier()

        def ml_phase(l, W):
            ppm = ppt[l % 2]
            pb_ = pbt[l % 2]
            wg_ = wgt[l % 2]
            Wq, Wk, Wv, Wo_, Wz, Wout = W
            with ExitStack() as st:
                Cst = p.sb("ml_C", [128, 512], F32, st)
                Cb = p.sb("ml_Cb", [128, 512], BF16, st)
                nst = p.sb("ml_n", [128, 4], F32, st)
                nb = p.sb("ml_nb", [128, 4], BF16, st)
                p.memset("pool", Cst[:], 0.0)
                p.memset("pool", Cb[:], 0.0)
                p.memset("pool", nst[:], 0.0)
                p.memset("pool", nb[:], 0.0)
                v3 = lambda ap: ap.rearrange("p (c g) -> p c g", g=8)
                glA = p.sb("ml_glA", [128, 128], F32, st)
                enA = p.sb("ml_enA", [128, 16, 4], F32, st)
                lfnA = p.sb("ml_lfnA", [128, 16, 4], F32, st)
                ebgA = p.sb("ml_ebgA", [128, 128], F32, st)
                lwA = p.sb("ml_lwA", [128, 16, 4], F32, st)
                ewA = p.sb("ml_ewA", [128, 16, 4], F32, st)
                ewbA = p.sb("ml_ewbA", [128, 16, 4], BF16, st)
                psgA = p.ps()
                for c in range(16):
                    tok_ = slice((c % 4) * 128, (c % 4 + 1) * 128)
                    for kc in range(8):
                        p.mm(psgA[:, c * 8:(c + 1) * 8], hn[kc][c // 4][:, tok_], wg_[:, kc, 0:8],
                             start=(kc == 0), stop=(kc == 7))
                p.tt("dve", glA[:], psgA[:, 0:128], pb_[:, PB_MLB:PB_MLB + 128], ALU.add)
                p.act(enA[:], v3(glA[:])[:, :, 4:8], AF.Exp, scale=-1.0)
                p.act(lfnA[:], enA[:], AF.Ln, bias=1.0)
                pscA = p.ps()
                for c in range(16):
                    p.mm(pscA[:, c * 8:c * 8 + 4], cf(C_U), lfnA[:, c, :])
                    p.mm(pscA[:, c * 8 + 4:c * 8 + 8], cf(C_ONES), lfnA[:, c, :])
                p.act(ebgA[:], pscA[:, 0:128], AF.Exp, scale=-1.0)
                p.tt("dve", lwA[:], v3(pscA[:, 0:128])[:, :, 0:4], v3(glA[:])[:, :, 0:4], ALU.add)
                p.act(ewA[:], lwA[:], AF.Exp)
                p.copy("pool", ewbA[:], ewA[:])
                qTs = [[p.sb(f"ml_qT{h}_{i}", [128, 512], BF16, st) for h in range(4)] for i in range(2)]
                kTs = [[p.sb(f"ml_kT{h}_{i}", [128, 512], BF16, st) for h in range(4)] for i in range(2)]
                yT = p.sb("ml_yT", [128, 4, 512], BF16, st)

                def ml_feat(t):
                    p.tag = f"ml_feat:{t}"
                    for h in range(4):
                        hs = slice(h * 128, (h + 1) * 128)
                        for (Wm, dstl, sc) in ((Wq, qTs[t % 2], 128 ** -0.5), (Wk, kTs[t % 2], 1.0)):
                            psx = p.ps()
                            for kc in range(8):
                                p.mm(psx[:], Wm[:, kc, hs], hn[kc][t][:], start=(kc == 0), stop=(kc == 7))
                            p.act(dstl[h][:], psx[:], AF.Copy, scale=sc)
                            yield

                def ml_proj(t, cc, out):
                    tok = slice(cc * 128, (cc + 1) * 128)
                    p.tag = f"ml_proj:{t * 4 + cc}"
                    def tokproj(Wm, ncol=512, c0=0):
                        ps_ = p.ps()
                        for kc in range(8):
                            p.mm(ps_[:, 0:ncol], hn[kc][t][:, tok], Wm[:, kc, c0:c0 + ncol],
                                 start=(kc == 0), stop=(kc == 7))
                        return ps_
                    pstk = p.ps()
                    pvk = pstk[:].bitcast(BF16)
                    for h in range(4):
                        p.tr(pvk[:, h * 128:(h + 1) * 128], kTs[t % 2][h][:, tok], cb(C_ID))
                    ktm = p.tmp(st, "ml_ktm", [128, 512], BF16, 4)
                    p.act(ktm[:], pvk[:, 0:512], AF.Copy)
                    yield
                    pso = tokproj(Wo_)
                    so = p.tmp(st, "ml_so", [128, 512], BF16, 4)
                    p.act(so[:], pso[:], AF.Tanh, scale=0.5)
                    yield
                    psz = tokproj(Wz)
                    sz = p.tmp(st, "ml_sz", [128, 512], F32, 2)
                    p.act(sz[:], psz[:], AF.Silu)
                    p.stt(so[:], so[:], 1.0, sz[:], ALU.add, ALU.mult)
                    yield
                    c = t * 4 + cc
                    psv = tokproj(Wv)
                    vaug = p.tmp(st, "ml_vaug", [128, 512], BF16, 4)
                    p.tt("dve", h4(vaug[:]), h4(psv[:]), bce(ewA[:, c, :]), ALU.mult)
                    out.update(dict(t=t, cc=cc, c=c, tok=tok, ktm=ktm, so=so, vaug=vaug))
                    yield

                def ml_core(u):
                    t, cc, c, tok, ktm, so, vaug = (u[k] for k in ("t", "cc", "c", "tok", "ktm", "so", "vaug"))
                    qT = qTs[t % 2]
                    kT = kTs[t % 2]
                    sm = p.tmp(st, "ml_sm", [128, 64], F32, 2)
                    den, dd, rr = (sm[:, 4 * i:4 * i + 4] for i in (6, 7, 8))
                    eb = ebgA[:, c * 8:c * 8 + 4]
                    eg = ebgA[:, c * 8 + 4:c * 8 + 8]
                    p.tag = f"ml_core:{t * 4 + cc}"
                    psp = p.ps()
                    PT = p.tmp(st, "ml_PT", [128, 512], BF16, 1)
                    for h in range(4):
                        hs = slice(h * 128, (h + 1) * 128)
                        p.mm(psp[:, hs], kT[h][:, tok], qT[h][:, tok])
                    p.tt("dve", h4(PT[:]), h4(psp[:]), bch(cf(C_U)), ALU.mult)
                    yield
                    psn = p.ps()
                    psd = p.ps()
                    for h in range(4):
                        hs = slice(h * 128, (h + 1) * 128)
                        p.mm(psn[:, hs], PT[:, hs], vaug[:, hs], start=True, stop=False)
                        p.mm(psn[:, hs], qT[h][:, tok], Cb[:, hs], start=False, stop=True)
                    for h in range(4):
                        hs = slice(h * 128, (h + 1) * 128)
                        p.mm(psd[:, h:h + 1], PT[:, hs], ewbA[:, c, h:h + 1], start=True, stop=False)
                        p.mm(psd[:, h:h + 1], qT[h][:, tok], nb[:, h:h + 1], start=False, stop=True)
                    p.tt("dve", den, psd[:, 0:4], eb, ALU.mult)
                    p.ts("dve", dd, den, -1.0, ALU.mult)
                    p.tt("dve", dd, dd, den, ALU.max)
                    p.ts("dve", dd, dd, 1.0, ALU.max)
                    p.op("dve", lambda e, dd=dd: e.reciprocal(dd, dd), [dd], [dd])
                    p.tt("dve", rr, dd, eb, ALU.mult)
                    hh = p.tmp(st, "ml_hh", [128, 512], F32, 1)
                    p.tt("dve", h4(hh[:]), h4(psn[:]), bce(rr), ALU.mult)
                    yield
                    hsq = p.tmp(st, "ml_hsq", [128, 512], F32, 1)
                    p.act(hsq[:], hh[:], AF.Square)
                    ssq = sm[:, 36:40]
                    p.op("dve", lambda e, ssq=ssq, hsq=hsq: e.tensor_reduce(
                        ssq, hsq[:].rearrange("p (h d) -> p h d", h=4), AX.X, ALU.add), [hsq], [ssq])
                    lnv = sm[:, 40:44]
                    rstd = sm[:, 44:48]
                    p.act(lnv, ssq, AF.Ln, bias=EPS, scale=1.0 / 128)
                    p.act(rstd, lnv, AF.Exp, scale=-0.5, bias=math.log(0.5))
                    ytm = p.tmp(st, "ml_ytm", [128, 512], BF16, 1)
                    for h in range(4):
                        hs = slice(h * 128, (h + 1) * 128)
                        p.stt(ytm[:, hs], hh[:, hs], sm[:, 44 + h:45 + h], so[:, hs], ALU.mult, ALU.mult)
                    yield
                    transpose_out(st, ytm, yT, tok, [ppm[:, PP_MLNW + h:PP_MLNW + h + 1] for h in range(4)])
                    yield
                    p.tag = f"ml_state:{t * 4 + cc}"
                    psk2 = p.ps()
                    psn2 = p.ps()
                    for h in range(4):
                        hs = slice(h * 128, (h + 1) * 128)
                        p.mm(psk2[:, hs], ktm[:, hs], vaug[:, hs])
                    for h in range(4):
                        hs = slice(h * 128, (h + 1) * 128)
                        p.mm(psn2[:, h:h + 1], ktm[:, hs], ewbA[:, c, h:h + 1])
                    ctmp = p.tmp(st, "ml_ctmp", [128, 512], F32, 1)
                    p.tt("dve", ctmp[:], psk2[:], Cst[:], ALU.add)
                    p.tt("dve", h4(Cst[:]), h4(ctmp[:]), bce(eg), ALU.mult)
                    p.act(Cb[:], Cst[:], AF.Copy)
                    ntmp = sm[:, 48:52]
                    p.tt("dve", ntmp, psn2[:, 0:4], nst[:], ALU.add)
                    p.tt("dve", nst[:], ntmp, eg, ALU.mult)
                    p.copy("pool", nb[:], nst[:])
                    yield

                def chain(*gs):
                    for g in gs:
                        yield from g

                def cores(us, t, last):
                    for u in us:
                        yield from ml_core(u)
                    if last:
                        out_proj(Wout, yT, t, st)
                        yield

                halves = [(t, hf) for t in range(NT) for hf in range(2)]
                prev = None
                for (t, hf) in halves:
                    cur = [dict(), dict()]
                    gs = []
                    if hf == 0:
                        gs.append(ml_feat(t))
                    gs += [ml_proj(t, 2 * hf, cur[0]), ml_proj(t, 2 * hf + 1, cur[1])]
                    if prev is None:
                        run_(chain(*gs))
                    else:
                        weave_(chain(*gs), cores(prev[0], prev[1], prev[2]))
                    prev = (cur, t, hf == 1)
                run_(cores(prev[0], prev[1], prev[2]))
                p.barrier()

        def gd_inverse(st, Nb, TpT):
            H = range(len(Nb))
            A = [p.tmp(st, f"gi_A{h}", [128, 256], BF16, 1) for h in H]
            NTt = [p.tmp(st, f"gi_NT{h}", [128, 128], BF16, 1) for h in H]
            DD = [p.tmp(st, f"gi_DD{h}", [128, 256], BF16, 1) for h in H]
            for h in H:
                p.tt("pool", A[h][:, 0:128], Nb[h][:], cb(C_NBD), ALU.mult)
            pst = [p.ps() for h in H]
            for h in H:
                pv = pst[h][:].bitcast(BF16)
                p.tr(pv[:, 0:128], A[h][:, 0:128], cb(C_ID))
                p.tr(pv[:, 128:256], Nb[h][:], cb(C_ID))
            for h in H:
                pv = pst[h][:].bitcast(BF16)
                p.act(A[h][:, 128:256], pv[:, 0:128], AF.Copy)
                p.copy("dve", NTt[h][:], pv[:, 128:256])
                p.tt("pool", DD[h][:], A[h][:], ii_b[:], ALU.add)
            yield
            def sq_mm(Pc):
                psq = [p.ps() for h in H]
                for h in H:
                    p.mm(psq[h][:, 0:128], Pc[h][:, 128:256], Pc[h][:, 0:128])
                    p.mm(psq[h][:, 128:256], Pc[h][:, 0:128], Pc[h][:, 128:256])
                return psq

            def sq_evac(k, psq):
                Pn = [p.tmp(st, f"gi_P{k % 2}_{h}", [128, 256], BF16, 1) for h in H]
                IP = [p.tmp(st, f"gi_IP{h}", [128, 256], BF16, 1) for h in H]
                for h in H:
                    if k < 3:
                        p.act(Pn[h][:], psq[h][:, 0:256], AF.Copy)
                    p.tt("dve", IP[h][:], psq[h][:, 0:256], ii_b[:], ALU.add)
                return Pn, IP
            psq = sq_mm(A)
            Pn, IP = sq_evac(1, psq)
            yield
            for k in range(1, 4):
                psd = [p.ps() for h in H]
                for h in H:
                    p.mm(psd[h][:, 0:128], IP[h][:, 128:256], DD[h][:, 0:128])
                    p.mm(psd[h][:, 128:256], IP[h][:, 0:128], DD[h][:, 128:256])
                if k < 3:
                    psq = sq_mm(Pn)
                for h in H:
                    if h % 2 == 0:
                        p.act(DD[h][:], psd[h][:, 0:256], AF.Copy)
                    else:
                        p.copy("dve", DD[h][:], psd[h][:, 0:256])
                if k < 3:
                    Pn, IP = sq_evac(k + 1, psq)
                yield
            for lv in range(3):
                Cm = [p.tmp(st, f"gi_IP{h}", [128, 256], BF16, 1) for h in H]
                for h in H:
                    p.tt("pool", Cm[h][:, 0:128], Nb[h][:], cb(C_CU0 + lv), ALU.mult)
                    if lv < 2:
                        p.tt("pool", Cm[h][:, 128:256], NTt[h][:], cb(C_CL0 + lv), ALU.mult)
                psw = [p.ps() for h in H]
                for h in H:
                    p.mm(psw[h][:, 0:128], Cm[h][:, 0:128], DD[h][:, 128:256])
                    if lv < 2:
                        p.mm(psw[h][:, 128:256], Cm[h][:, 128:256], DD[h][:, 0:128])
                WW = [p.tmp(st, f"gi_P0_{h}", [128, 256], BF16, 1) for h in H]
                n = 256 if lv < 2 else 128
                for h in H:
                    p.act(WW[h][:, 0:n], psw[h][:, 0:n], AF.Copy)
                yield
                psz = [p.ps() for h in H]
                for h in H:
                    p.mm(psz[h][:, 0:128], WW[h][:, 0:128], DD[h][:, 0:128])
                    if lv < 2:
                        p.mm(psz[h][:, 128:256], WW[h][:, 128:256], DD[h][:, 128:256])
                for h in H:
                    if lv < 2:
                        p.tt("dve", DD[h][:], DD[h][:], psz[h][:, 0:256], ALU.subtract)
                    else:
                        p.tt("dve", TpT[h], DD[h][:, 0:128], psz[h][:, 0:128], ALU.subtract)
                yield

        def gd_phase(l, W, scratch):
            pp_ = ppt[l % 2]
            pb_ = pbt[l % 2]
            wg_ = wgt[l % 2]
            Wq, Wk, Wv, Wz, Wout = W
            with ExitStack() as st:
                Sst = p.sb("gd_S", [128, 512], F32, st)
                Sb = p.sb("gd_Sb", [128, 512], BF16, st)
                p.memset("pool", Sst[:], 0.0)
                p.memset("pool", Sb[:], 0.0)
                tails = [p.sb(f"gd_tail{i}", [128, 3], F32, st) for i in range(12)]
                for i in range(12):
                    p.memset("pool", tails[i][:], 0.0)
                v3 = lambda ap: ap.rearrange("p (c g) -> p c g", g=8)
                w3 = lambda ap: ap.rearrange("p (c g) -> p c g", g=4)
                expA = p.sb("gd_expA", [128, 16, 4], F32, st)
                p.act(expA[:], w3(pb_[:, PB_ALOG:PB_ALOG + 64]), AF.Exp)
                betaA = p.sb("gd_betaA", [128, 16, 4], F32, st)
                xxA = p.sb("gd_xxA", [128, 16, 4], F32, st)
                axA = p.sb("gd_axA", [128, 16, 4], F32, st)
                enA = p.sb("gd_enA", [128, 16, 4], F32, st)
                gnegA = p.sb("gd_gnegA", [128, 16, 4], F32, st)
                csA = p.sb("gd_csA", [128, 128], F32, st)
                egA = p.sb("gd_egA", [128, 128], F32, st)
                ekdA = p.sb("gd_ekdA", [128, 16, 4], F32, st)
                negegcA = p.sb("gd_negegcA", [128, 16, 4], F32, st)
                psgA = p.ps()
                for c in range(16):
                    tok_ = slice((c % 4) * 128, (c % 4 + 1) * 128)
                    for kc in range(8):
                        p.mm(psgA[:, c * 8:(c + 1) * 8], hn[kc][c // 4][:, tok_], wg_[:, kc, 8:16],
                             start=(kc == 0), stop=(kc == 7))
                p.act(betaA[:], v3(psgA[:, 0:128])[:, :, 4:8], AF.Sigmoid)
                p.tt("dve", xxA[:], v3(psgA[:, 0:128])[:, :, 0:4], w3(pb_[:, PB_DTB:PB_DTB + 64]), ALU.add)
                p.ts("dve", axA[:], xxA[:], -1.0, ALU.mult)
                p.tt("dve", axA[:], axA[:], xxA[:], ALU.max)
                p.act(enA[:], axA[:], AF.Exp, scale=-1.0)
                p.act(enA[:], enA[:], AF.Ln, bias=1.0)
                p.ts("dve", axA[:], xxA[:], 0.0, ALU.max)
                p.tt("dve", enA[:], enA[:], axA[:], ALU.add)
                p.tt("dve", gnegA[:], enA[:], expA[:], ALU.mult)
                pscA = p.ps()
                for c in range(16):
                    p.mm(pscA[:, c * 8:c * 8 + 4], cf(C_U), gnegA[:, c, :])
                    p.mm(pscA[:, c * 8 + 4:c * 8 + 8], cf(C_ONES), gnegA[:, c, :])
                p.act(csA[:], pscA[:, 0:128], AF.Copy)
                p.act(egA[:], pscA[:, 0:128], AF.Exp, scale=-1.0)
                p.tt("pool", ekdA[:], v3(csA[:])[:, :, 0:4], v3(csA[:])[:, :, 4:8], ALU.subtract)
                p.act(ekdA[:], ekdA[:], AF.Exp)
                p.ts("pool", negegcA[:], v3(egA[:])[:, :, 0:4], -1.0, ALU.mult)
                yT = p.sb("gd_yT", [128, 4, 512], BF16, st)
                gd_tmpo = p.sb("gd_tmpo", [128, 512], F32, st)
                qkT = [p.sb(f"gd_qkT{h}", [128, 2, 512], BF16, st) for h in range(4)]
                vT = [p.sb(f"gd_vT{h}", [128, 512], BF16, st) for h in range(4)]
                H4 = range(4)
                HS = [slice(h * 128, (h + 1) * 128) for h in H4]

                def cbufs(c):
                    b = c % 2
                    base = b * 2048
                    mk = lambda nm, o, n: V(scratch[:, base + o:base + o + n], f"gdscr_{nm}_{b}")
                    return dict(kdec=mk("kdec", 0, 512), vtm=mk("vtm", 512, 512), aqkT=mk("aqkT", 1024, 512),
                                TpT=[mk(f"T{h}", 1536 + h * 128, 128) for h in H4])

                def gd_zburst(sc_, t):
                    pzs = []
                    for cc in range(4):
                        tok = slice(cc * 128, (cc + 1) * 128)
                        psz = p.ps()
                        for kc in range(8):
                            p.mm(psz[:], hn[kc][t][:, tok], Wz[:, kc, :], start=(kc == 0), stop=(kc == 7))
                        pzs.append(psz)
                    szs = []
                    for cc in range(4):
                        sz = p.tmp(sc_, "gd_sz", [128, 512], BF16, 4)
                        p.act(sz[:], pzs[cc][:], AF.Silu)
                        szs.append(sz)
                    return szs

                def gd_part1(sc_, t, cc, out, sz):
                    tok = slice(cc * 128, (cc + 1) * 128)
                    c = t * 4 + cc
                    B = cbufs(c)
                    p.tag = f"gd_prep:{c}"
                    pst = p.ps()
                    pv = pst[:].bitcast(BF16)
                    for h in H4:
                        p.tr(pv[:, HS[h]], qkT[h][:, 0, tok], cb(C_ID))
                    kdec = B["kdec"]
                    p.tt("dve", V(h4(kdec.ap), kdec.key), h4(pv[:, 0:512]), bce(ekdA[:, c, :]), ALU.mult)
                    yield
                    pst2 = p.ps()
                    pv2 = pst2[:].bitcast(BF16)
                    for h in H4:
                        p.tr(pv2[:, HS[h]], vT[h][:, tok], cb(C_ID))
                    vtm = B["vtm"]
                    p.copy("dve", vtm, pv2[:, 0:512])
                    yield
                    p.tag = f"gd_gam:{c}"
                    GU = p.tmp(sc_, "gd_GU", [128, 512], F32, 1)
                    for h in H4:
                        p.act(GU[:, HS[h]], cf(C_U), AF.Copy, scale=gnegA[:, c, h:h + 1])
                    psG = p.ps()
                    for h in H4:
                        p.mm(psG[:, HS[h]], cf(C_ID), cf(C_POS), start=True, stop=False)
                        p.mm(psG[:, HS[h]], cf(C_ONES), GU[:, HS[h]], start=False, stop=True)
                    gamT = p.tmp(sc_, "gd_gamT", [128, 512], F32, 1)
                    for h in H4:
                        p.act(gamT[:, HS[h]], psG[:, HS[h]], AF.Exp, scale=-1.0, bias=csA[:, c * 8 + h:c * 8 + h + 1])
                    yield
                    gSU = gamT
                    aqkT = B["aqkT"]
                    Nb = [p.tmp(sc_, f"gd_Nb{h}", [128, 128], BF16, 1) for h in H4]
                    for h in H4:
                        psK = p.ps()
                        p.mm(psK[:, 0:256].rearrange("p (a t) -> p a t", a=2), qkT[h][:, 0, tok], qkT[h][:, :, tok])
                        p.tt("dve", aqkT[:, HS[h]], psK[:, 128:256], gamT[:, HS[h]], ALU.mult)
                        p.stt(Nb[h][:], psK[:, 0:128], betaA[:, c, h:h + 1], gSU[:, HS[h]], ALU.mult, ALU.mult)
                    yield
                    p.tag = f"gd_inv:{c}"
                    yield from gd_inverse(sc_, Nb, B["TpT"])
                    out.update(B)
                    out.update(sz=sz, t=t, cc=cc, c=c, tok=tok)

                def gd_part2(sc_, u, yT):
                    t, cc, c, tok = u["t"], u["cc"], u["c"], u["tok"]
                    kdec, vtm, aqkT, TpT, sz = u["kdec"], u["vtm"], u["aqkT"], u["TpT"], u["sz"]
                    p.tag = f"gd_state:{c}"
                    psAk = p.ps()
                    psAq = p.ps()
                    for h in H4:
                        p.mm(psAk[:, HS[h]], qkT[h][:, 0, tok], Sb[:, HS[h]])
                        p.mm(psAq[:, HS[h]], qkT[h][:, 1, tok], Sb[:, HS[h]])
                    Rm = p.tmp(sc_, "gd_Rm", [128, 512], BF16, 1)
                    qSe = p.tmp(sc_, "gd_qSe", [128, 512], F32, 1)
                    vnew = p.tmp(sc_, "gd_vnew", [128, 512], BF16, 1)
                    for h in H4:
                        p.stt(Rm[:, HS[h]], psAk[:, HS[h]], negegcA[:, c, h:h + 1], vtm[:, HS[h]], ALU.mult, ALU.add)
                    p.tt("dve", h4(qSe[:]), h4(psAq[:]), bce(egA[:, c * 8:c * 8 + 4]), ALU.mult)
                    yield
                    psV = p.ps()
                    for h in H4:
                        p.mm(psV[:, HS[h]], TpT[h], Rm[:, HS[h]])
                    p.tt("dve", h4(vnew[:]), h4(psV[:]), bce(betaA[:, c, :]), ALU.mult)
                    yield
                    oall = p.tmp(sc_, "gd_oall", [128, 512], F32, 1)
                    psO = p.ps()
                    psS = p.ps()
                    for h in H4:
                        p.mm(psO[:, HS[h]], aqkT[:, HS[h]], vnew[:, HS[h]])
                        p.mm(psS[:, HS[h]], kdec[:, HS[h]], vnew[:, HS[h]])
                    p.tt("dve", h4(Sst[:]), h4(Sst[:]), bce(egA[:, c * 8 + 4:c * 8 + 8]), ALU.mult)
                    p.tt("dve", Sst[:], Sst[:], psS[:], ALU.add)
                    p.tt("dve", oall[:], psO[:], qSe[:], ALU.add)
                    p.act(Sb[:], Sst[:], AF.Copy)
                    yield
                    p.tag = f"gd_out:{c}"
                    sm = p.tmp(sc_, "gd_sm", [128, 64], F32, 2)
                    osq = p.tmp(sc_, "gd_osq", [128, 512], BF16, 1)
                    p.act(osq[:], oall[:], AF.Square)
                    ssq = sm[:, 48:52]
                    p.op("dve", lambda e, ssq=ssq, osq=osq: e.tensor_reduce(
                        ssq, osq[:].rearrange("p (h d) -> p h d", h=4), AX.X, ALU.add), [osq], [ssq])
                    yield
                    lnv2 = sm[:, 52:56]
                    p.act(lnv2, ssq, AF.Ln, bias=EPS, scale=1.0 / 128)
                    p.act(sm[:, 56:60], lnv2, AF.Exp, scale=-0.5)
                    ytm = p.tmp(sc_, "gd_ytm", [128, 512], BF16, 1)
                    for h in H4:
                        p.stt(ytm[:, HS[h]], oall[:, HS[h]], sm[:, 56 + h:57 + h], sz[:, HS[h]], ALU.mult, ALU.mult)
                    yield
                    transpose_out(sc_, ytm, yT, tok, [pp_[:, PP_GDNW:PP_GDNW + 1]])
                    yield

                def run(g):
                    for _ in g:
                        pass

                def weave(g1, g2):
                    a = b = True
                    while a or b:
                        for _ in range(2):
                            if a:
                                try:
                                    next(g1)
                                except StopIteration:
                                    a = False
                        if b:
                            try:
                                next(g2)
                            except StopIteration:
                                b = False

                for t in range(NT):
                    p.tag = f"gd_feat:{t}"
                    with ExitStack() as sf:
                        def stage_a(pj, h):
                            Wm = (Wq, Wk, Wv)[pj]
                            hs = slice(h * 128, (h + 1) * 128)
                            psx = p.ps()
                            for kc in range(8):
                                p.mm(psx[:], Wm[:, kc, hs], hn[kc][t][:], start=(kc == 0), stop=(kc == 7))
                            i = pj * 4 + h
                            xc = conv4(sf, psx, tails[i],
                                       lambda k, i=i: pp_[:, PP_GCW + i * 4 + k:PP_GCW + i * 4 + k + 1], None, "gd", 4)
                            return (pj, h, xc, conv4.xe)

                        def stage_b(us):
                            for (pj, h, xc, xe) in us:
                                if pj == 2:
                                    p.act(vT[h][:], xc[:], AF.Silu)
                                else:
                                    p.act(xc[:], xc[:], AF.Silu)
                            pend = []
                            for (pj, h, xc, xe) in us:
                                if pj == 2:
                                    continue
                                sqb = p.tmp(sf, "gd_sqb", [128, 512], BF16, 2)
                                p.act(sqb[:], xc[:], AF.Square)
                                pss = p.ps()
                                p.mm(pss[:], cb(C_ONES), sqb[:])
                                pend.append((pj, h, xc, xe, pss))
                            for (pj, h, xc, xe, pss) in pend:
                                p.act(xe[:, 0:512], pss[:], AF.Ln, bias=EPS)
                                bias = -0.5 * math.log(128.0) if pj == 0 else 0.0
                                p.act(xe[:, 0:512], xe[:, 0:512], AF.Exp, scale=-0.5, bias=bias)
                            for (pj, h, xc, xe, pss) in pend:
                                p.tt("dve", qkT[h][:, 1 - pj, :], xc[:], xe[:, 0:512], ALU.mult)
                        units = [(pj, h) for pj in range(3) for h in range(4)]

                        def feat_gen():
                            prev = None
                            for i2 in range(0, 12, 2):
                                cur = [stage_a(*units[i2])]
                                yield
                                cur.append(stage_a(*units[i2 + 1]))
                                yield
                                if prev is not None:
                                    stage_b(prev)
                                    yield
                                prev = cur
                            stage_b(prev)
                            yield
                        if t == 0:
                            run_(feat_gen())
                        else:
                            p.tag = f"gd_feat:{t}"
                            weave_(feat_gen(), out_proj_g(Wout, yT, t - 1, gd_tmpo))
                        p.barrier()
                    with ExitStack() as sc_:
                        us = [dict() for _ in range(4)]
                        szs = gd_zburst(sc_, t)
                        run(gd_part1(sc_, t, 0, us[0], szs[0]))
                        for cc in range(4):
                            if cc + 1 < 4:
                                weave(gd_part1(sc_, t, cc + 1, us[cc + 1], szs[cc + 1]), gd_part2(sc_, us[cc], yT))
                            else:
                                run(gd_part2(sc_, us[cc], yT))
                        p.barrier()
                p.tag = "gd_oproj:3"
                run_(out_proj_g(Wout, yT, NT - 1, gd_tmpo))
                p.barrier()

        groups = []
        for l in range(n_layers):
            if "rg" in phases:
                groups += [(l, "in", COLS["rg_x"]), (l, "in", COLS["rg_z"]), (l, "out", 0)]
            if "ml" in phases:
                groups += [(l, "in", COLS[k]) for k in ("ml_q", "ml_k", "ml_v", "ml_o", "ml_z")] + [(l, "out", 1)]
            if "gd" in phases:
                groups += [(l, "in", COLS[k]) for k in ("gd_q", "gd_k", "gd_v", "gd_z")] + [(l, "out", 2)]
        gpos = [0]
        loaded = []

        def prefetch(upto):
            while gpos[0] < min(upto, len(groups)):
                loaded.append(load_group(*groups[gpos[0]]))
                gpos[0] += 1

        def take(n):
            prefetch(len(loaded_used) + n)
            r = loaded[len(loaded_used):len(loaded_used) + n]
            loaded_used.extend(r)
            return r
        loaded_used = []

        load_small(0)
        prefetch(NSLOT)
        for l in range(n_layers):
            p.epoch = l + 1
            if l + 1 < n_layers:
                load_small(l + 1)
            load_small2(l)
            with ExitStack() as st:
                pp_ = ppt[l % 2]
                rms_to(lambda c, t: hn[c][t][:], None, lambda c: pp_[:, PP_NW + c:PP_NW + c + 1], st)
                p.barrier()
            if "rg" in phases:
                W = take(3)
                rg_phase(l, W)
                prefetch(len(loaded_used) + NSLOT)
            if "ml" in phases:
                W = take(6)
                ml_phase(l, W)
                prefetch(len(loaded_used) + (5 if "gd" in phases else NSLOT))
            if "gd" in phases:
                W = take(5)
                scratch = ring[ring_pos[0] % NSLOT]
                gd_phase(l, W, scratch)
                prefetch(len(loaded_used) + NSLOT)

        finals = []
        with ExitStack() as st:
            if final_norm:
                pp_ = ppt[(n_layers - 1) % 2]
                outb = {}

                def dst(c, t):
                    o = p.tmp(st, "fin_o", [128, 512], F32, 4)
                    outb[(c, t)] = o
                    return o[:]
                for t in range(NT):
                    ss = p.ps()
                    for c in range(8):
                        sq = p.tmp(st, "n_sq", [128, 512], BF16, 3)
                        p.act(sq[:], xT[c][t][:], AF.Square)
                        p.mm(ss[:], cb(C_ONES), sq[:], start=(c == 0), stop=(c == 7))
                    lnv = p.tmp(st, "n_ln", [128, 512], F32, 2)
                    p.act(lnv[:], ss[:], AF.Ln, bias=EPS, scale=1.0 / D)
                    rstd = p.tmp(st, "n_rs", [128, 512], F32, 2)
                    p.act(rstd[:], lnv[:], AF.Exp, scale=-0.5)
                    for c in range(8):
                        o = p.tmp(st, "fin_o", [128, 512], F32, 4)
                        p.stt(o[:], xT[c][t][:], pp_[:, PP_FNW + c:PP_FNW + c + 1], rstd[:], ALU.mult, ALU.mult)
                        tk = p.dma("sp", f"o{(c + t * 8) % 4}", y_d[c, :, t * 512:(t + 1) * 512], o[:])
                        finals.append(tk)
            else:
                for t in range(NT):
                    for c in range(8):
                        tk = p.dma("sp", f"o{(c + t * 8) % 4}", y_d[c, :, t * 512:(t + 1) * 512], xT[c][t][:])
                        finals.append(tk)
            fw = {}
            for tk in finals:
                fw[tk[:3]] = max(fw.get(tk[:3], 0), tk[3])
            p.finalize(list(fw.items()))
    return nc, p


_CACHE = {}


def _get(n_layers, final_norm):
    k = (n_layers, final_norm)
    if k not in _CACHE:
        _CACHE[k] = build(n_layers, final_norm)[0]
    return _CACHE[k]


def kernel(**inp):
    inp = {k: np.asarray(v) for k, v in inp.items()}
    x = inp["x"].astype(np.float32)
    B = x.shape[0]
    pp, pb, gw, wg = pack_params(inp)
    cst = make_consts()
    nc = _get(DEPTH, True)
    w_in = np.ascontiguousarray(inp["w_in"], dtype=np.float32)
    w_out = np.ascontiguousarray(inp["w_out"], dtype=np.float32)
    in_maps = []
    for b in range(B):
        xt = np.ascontiguousarray(x[b].T.reshape(8, 128, S))
        in_maps.append(dict(x=xt, w_in=w_in, w_out=w_out, pp=pp, pb=pb, gw=gw, wg=wg, cst=cst))
    res = run_bass_kernel_spmd(nc, in_maps, core_ids=list(range(B)))
    out = np.stack([np.asarray(r["y"]).reshape(D, S).T for r in res.results], axis=0)
    return np.ascontiguousarray(out.astype(np.float32))
```
